# Optimizing a Trainium2 kernel written in Bass

```python
import jax, jax.numpy as jnp
from jax import lax
import numpy as np

D_MODEL = 1024
BATCH = 32
SEQ = 2048
DEPTH = 1
DEC_BATCH = 8
DEC_SEQ = 16
PAST_LEN = 2048

CHUNK = 64
HG_HEADS = 4
HG_DK = 128
HG_DV = 128
HG_WIDTH = HG_HEADS * HG_DV
ATTN_HEADS = 8
ATTN_KV_HEADS = 2
ATTN_HEAD_DIM = 64
ATTN_GROUP = ATTN_HEADS // ATTN_KV_HEADS
ATTN_WIDTH = ATTN_HEADS * ATTN_HEAD_DIM
MIX_WIDTH = HG_WIDTH + ATTN_WIDTH
WINDOW = 128
BAND = WINDOW // CHUNK + 1
MEM_TOKENS = 256
MEM_HEADS = 4
MEM_HEAD_DIM = D_MODEL // MEM_HEADS
D_FF = -(-8 * D_MODEL // (3 * 256)) * 256
IN_SPLITS = (HG_HEADS * HG_DK, HG_HEADS * HG_DK, HG_WIDTH, HG_WIDTH,
             ATTN_WIDTH, ATTN_KV_HEADS * ATTN_HEAD_DIM, ATTN_KV_HEADS * ATTN_HEAD_DIM)
IN_COLS = (2 * HG_HEADS * HG_DK + 2 * HG_WIDTH + ATTN_WIDTH
           + 2 * ATTN_KV_HEADS * ATTN_HEAD_DIM)
ALPHA = (2.0 * DEPTH) ** 0.25
BETA = (8.0 * DEPTH) ** -0.25
NEG = -1e30

kernel_name = "hymba_hgrn2_swa_sink_alibi_deepnorm_stream"


def layer_norm(x, g, b, eps=1e-5):
    xf = x.astype(jnp.float32)
    mu = xf.mean(-1, keepdims=True)
    var = jnp.square(xf - mu).mean(-1, keepdims=True)
    return ((xf - mu) * lax.rsqrt(var + eps) * g.astype(jnp.float32)
            + b.astype(jnp.float32)).astype(x.dtype)


def rms_norm(x, g, eps=1e-6):
    xf = x.astype(jnp.float32)
    return xf * lax.rsqrt(jnp.mean(jnp.square(xf), -1, keepdims=True) + eps) * g.astype(jnp.float32)


def split_in(z):
    return jnp.split(z, list(np.cumsum(IN_SPLITS)[:-1]), axis=-1)


def gla_chunkwise(q, k, v, logf, S0, L):
    B, T, H, dk = q.shape
    dv = v.shape[-1]
    N = T // L
    blk = lambda a: jnp.moveaxis(a.reshape(B, N, L, H, a.shape[-1]), 1, 0)
    qr, kr, vr = blk(q), blk(k), blk(v)
    Gr = jnp.cumsum(blk(logf), axis=2)
    mask = jnp.tril(jnp.ones((L, L), bool))

    def step(S, inp):
        qc, kc, vc, Gc = inp
        qg = qc * jnp.exp(Gc)
        kg = kc * jnp.exp(-Gc)
        A = jnp.where(mask, jnp.einsum('blhk,bshk->bhls', qg, kg), 0.0)
        o = jnp.einsum('bhls,bshv->blhv', A, vc) + jnp.einsum('blhk,bhkv->blhv', qg, S)
        GL = Gc[:, -1]
        kdec = kc * jnp.exp(GL[:, None] - Gc)
        S = jnp.exp(GL)[..., None] * S + jnp.einsum('bshk,bshv->bhkv', kdec, vc)
        return S, o

    S, o = lax.scan(step, S0, (qr, kr, vr, Gr))
    return jnp.moveaxis(o, 0, 1).reshape(B, T, H, dv), S


def hgrn_mixer(hq, hf, hi, hg, lb, g_norm, S0):
    B, T, _ = hq.shape
    hshape = (B, T, HG_HEADS, HG_DK)
    f = lb + (1.0 - lb) * jax.nn.sigmoid(hf.astype(jnp.float32))
    q = jax.nn.silu(hq.astype(jnp.float32)).reshape(hshape)
    k = (1.0 - f).reshape(hshape)
    logf = jnp.log(f).reshape(hshape)
    v = hi.astype(jnp.float32).reshape(B, T, HG_HEADS, HG_DV)
    o, S = gla_chunkwise(q, k, v, logf, S0.astype(jnp.float32), min(CHUNK, T))
    o = rms_norm(o, g_norm) * jax.nn.silu(hg.astype(jnp.float32)).reshape(B, T, HG_HEADS, HG_DV)
    return o.reshape(B, T, HG_WIDTH).astype(hq.dtype), S


def alibi_bias(qpos, kpos):
    slopes = jnp.asarray(2.0 ** (-8.0 * np.arange(1, ATTN_HEADS + 1) / ATTN_HEADS), jnp.float32)
    dist = jnp.abs(qpos[:, None] - kpos[None, :]).astype(jnp.float32)
    return -slopes.reshape(ATTN_KV_HEADS, ATTN_GROUP, 1, 1) * dist


def sink_attention(q, k, v, bias, valid, sinks):
    s = jnp.einsum('...qhgd,...khd->...hgqk', q, k).astype(jnp.float32) * ATTN_HEAD_DIM ** -0.5 + bias
    if valid is not None:
        s = jnp.where(valid, s, NEG)
    sink = jnp.broadcast_to(sinks.astype(jnp.float32).reshape(ATTN_KV_HEADS, ATTN_GROUP, 1, 1),
                            s.shape[:-1] + (1,))
    p = jax.nn.softmax(jnp.concatenate([s, sink], axis=-1), axis=-1)[..., :-1]
    return jnp.einsum('...hgqk,...khd->...qhgd', p.astype(v.dtype), v)


def band_blocks(a):
    B, T = a.shape[:2]
    N = T // CHUNK
    ac = a.reshape(B, N, CHUNK, ATTN_KV_HEADS, ATTN_HEAD_DIM)
    ap = jnp.concatenate([jnp.zeros((B, BAND - 1) + ac.shape[2:], a.dtype), ac], axis=1)
    blocks = jnp.stack([ap[:, i:i + N] for i in range(BAND)], axis=2)
    return blocks.reshape(B, N, BAND * CHUNK, ATTN_KV_HEADS, ATTN_HEAD_DIM)


def swa_prompt(aq, ak, av, sinks):
    B, T, _ = aq.shape
    N = T // CHUNK
    q = aq.reshape(B, N, CHUNK, ATTN_KV_HEADS, ATTN_GROUP, ATTN_HEAD_DIM)
    k_rows = ak.reshape(B, T, ATTN_KV_HEADS, ATTN_HEAD_DIM)
    v_rows = av.reshape(B, T, ATTN_KV_HEADS, ATTN_HEAD_DIM)
    kb, vb = band_blocks(k_rows), band_blocks(v_rows)
    bias = alibi_bias(jnp.arange(CHUNK) + (BAND - 1) * CHUNK, jnp.arange(BAND * CHUNK))
    key_chunk = jnp.arange(N)[:, None] - (BAND - 1) + jnp.arange(BAND * CHUNK)[None, :] // CHUNK
    valid = (key_chunk >= 0)[:, None, None, None, :]
    o = sink_attention(q, kb, vb, bias, valid, sinks)
    W = min(WINDOW, T)
    return o.reshape(B, T, ATTN_WIDTH), k_rows[:, T - W:], v_rows[:, T - W:]


def swa_sample(aq, ak, av, k_cache, v_cache, sinks):
    B, T, _ = aq.shape
    W = k_cache.shape[1]
    q = aq.reshape(B, T, ATTN_KV_HEADS, ATTN_GROUP, ATTN_HEAD_DIM)
    k_new = ak.reshape(B, T, ATTN_KV_HEADS, ATTN_HEAD_DIM)
    v_new = av.reshape(B, T, ATTN_KV_HEADS, ATTN_HEAD_DIM)
    k = jnp.concatenate([k_cache.astype(k_new.dtype), k_new], axis=1)
    v = jnp.concatenate([v_cache.astype(v_new.dtype), v_new], axis=1)
    bias = alibi_bias(W + jnp.arange(T), jnp.arange(W + T))
    o = sink_attention(q, k, v, bias, None, sinks)
    return o.reshape(B, T, ATTN_WIDTH), k_new, v_new


def mem_kv(mem, w_mem_kv):
    B, M, _ = mem.shape
    mk, mv = jnp.split(mem @ w_mem_kv, 2, axis=-1)
    return (mk.reshape(B, M, MEM_HEADS, MEM_HEAD_DIM), mv.reshape(B, M, MEM_HEADS, MEM_HEAD_DIM))


def mem_attention(x, mk, mv, w_q, w_o):
    B, T, _ = x.shape
    q = (x @ w_q).reshape(B, T, MEM_HEADS, MEM_HEAD_DIM)
    s = jnp.einsum('bthd,bmhd->bhtm', q, mk.astype(q.dtype)).astype(jnp.float32) * MEM_HEAD_DIM ** -0.5
    p = jax.nn.softmax(s, axis=-1)
    o = jnp.einsum('bhtm,bmhd->bthd', p.astype(x.dtype), mv.astype(x.dtype)).reshape(B, T, D_MODEL)
    return o @ w_o


def swiglu(x, w_in, w_out):
    g, u = jnp.split(x @ w_in, 2, axis=-1)
    return (jax.nn.silu(g) * u) @ w_out


def post_blocks(x, mix, mk, mv, w_mem_q, w_mem_o, w_ffn_in, w_ffn_out, ln_g, ln_b):
    x = layer_norm(ALPHA * x + mix, ln_g[0], ln_b[0])
    x = layer_norm(ALPHA * x + mem_attention(x, mk, mv, w_mem_q, w_mem_o), ln_g[1], ln_b[1])
    x = layer_norm(ALPHA * x + swiglu(x, w_ffn_in, w_ffn_out), ln_g[2], ln_b[2])
    return x


def setup_inputs(seed: int = 0) -> dict:
    key = jax.random.key(seed)
    ks = jax.random.split(key, 20)
    nrm = lambda k, shape, scale: jax.random.normal(k, shape, jnp.float32) * scale
    W = min(WINDOW, PAST_LEN)
    return {
        "x_prompt": nrm(ks[0], (BATCH, SEQ, D_MODEL), 1.0),
        "x_sample": nrm(ks[1], (DEC_BATCH, DEC_SEQ, D_MODEL), 1.0),
        "cache_swa_k": nrm(ks[2], (DEPTH, DEC_BATCH, W, ATTN_KV_HEADS, ATTN_HEAD_DIM), 1.0),
        "cache_swa_v": nrm(ks[3], (DEPTH, DEC_BATCH, W, ATTN_KV_HEADS, ATTN_HEAD_DIM), 1.0),
        "state_hgrn": nrm(ks[4], (DEPTH, DEC_BATCH, HG_HEADS, HG_DK, HG_DV), 0.5),
        "cache_mem_k": nrm(ks[5], (DEPTH, DEC_BATCH, MEM_TOKENS, MEM_HEADS, MEM_HEAD_DIM), 1.0),
        "cache_mem_v": nrm(ks[6], (DEPTH, DEC_BATCH, MEM_TOKENS, MEM_HEADS, MEM_HEAD_DIM), 1.0),
        "mem_prompt": nrm(ks[7], (BATCH, MEM_TOKENS, D_MODEL), 1.0),
        "w_in": nrm(ks[8], (DEPTH, D_MODEL, IN_COLS), D_MODEL ** -0.5),
        "hgrn_lb_logits": 1.0 + nrm(ks[9], (DEPTH + 1, HG_HEADS * HG_DK), 0.1),
        "hgrn_norm_g": 1.0 + nrm(ks[10], (DEPTH, HG_DV), 0.02),
        "attn_sinks": nrm(ks[11], (DEPTH, ATTN_HEADS), 0.5),
        "w_out": nrm(ks[12], (DEPTH, MIX_WIDTH, D_MODEL), MIX_WIDTH ** -0.5 * BETA),
        "w_mem_q": nrm(ks[13], (DEPTH, D_MODEL, D_MODEL), D_MODEL ** -0.5),
        "w_mem_kv": nrm(ks[14], (DEPTH, D_MODEL, 2 * D_MODEL), D_MODEL ** -0.5),
        "w_mem_o": nrm(ks[15], (DEPTH, D_MODEL, D_MODEL), D_MODEL ** -0.5 * BETA),
        "w_ffn_in": nrm(ks[16], (DEPTH, D_MODEL, 2 * D_FF), D_MODEL ** -0.5),
        "w_ffn_out": nrm(ks[17], (DEPTH, D_FF, D_MODEL), D_FF ** -0.5 * BETA),
        "ln_g": 1.0 + nrm(ks[18], (DEPTH, 3, D_MODEL), 0.02),
        "ln_b": nrm(ks[19], (DEPTH, 3, D_MODEL), 0.02),
    }


def reference(x_prompt, x_sample, cache_swa_k, cache_swa_v, state_hgrn, cache_mem_k, cache_mem_v,
              mem_prompt, w_in, hgrn_lb_logits, hgrn_norm_g, attn_sinks, w_out, w_mem_q, w_mem_kv,
              w_mem_o, w_ffn_in, w_ffn_out, ln_g, ln_b):
    lb_all = jnp.cumsum(jax.nn.softmax(hgrn_lb_logits.astype(jnp.float32), axis=0), axis=0)
    yp, ys = x_prompt, x_sample
    kp_l, vp_l, sp_l, mkp_l, mvp_l, ks_l, vs_l, ss_l = [], [], [], [], [], [], [], []
    for l in range(DEPTH):
        hq, hf, hi, hg, aq, ak, av = split_in(yp @ w_in[l])
        S0 = jnp.zeros((yp.shape[0], HG_HEADS, HG_DK, HG_DV), jnp.float32)
        o_h, S_p = hgrn_mixer(hq, hf, hi, hg, lb_all[l], hgrn_norm_g[l], S0)
        o_a, k_p, v_p = swa_prompt(aq, ak, av, attn_sinks[l])
        mix = jnp.concatenate([o_h, o_a], axis=-1) @ w_out[l]
        mk_p, mv_p = mem_kv(mem_prompt, w_mem_kv[l])
        yp = post_blocks(yp, mix, mk_p, mv_p, w_mem_q[l], w_mem_o[l], w_ffn_in[l], w_ffn_out[l],
                         ln_g[l], ln_b[l])
        hq, hf, hi, hg, aq, ak, av = split_in(ys @ w_in[l])
        o_h, S_s = hgrn_mixer(hq, hf, hi, hg, lb_all[l], hgrn_norm_g[l], state_hgrn[l])
        o_a, k_s, v_s = swa_sample(aq, ak, av, cache_swa_k[l], cache_swa_v[l], attn_sinks[l])
        mix = jnp.concatenate([o_h, o_a], axis=-1) @ w_out[l]
        ys = post_blocks(ys, mix, cache_mem_k[l], cache_mem_v[l], w_mem_q[l], w_mem_o[l],
                         w_ffn_in[l], w_ffn_out[l], ln_g[l], ln_b[l])
        kp_l.append(k_p); vp_l.append(v_p); sp_l.append(S_p)
        mkp_l.append(mk_p); mvp_l.append(mv_p)
        ks_l.append(k_s); vs_l.append(v_s); ss_l.append(S_s)
    new_swa_k_prompt = jnp.stack(kp_l)
    new_swa_v_prompt = jnp.stack(vp_l)
    new_hgrn_state_prompt = jnp.stack(sp_l)
    new_mem_k_prompt = jnp.stack(mkp_l)
    new_mem_v_prompt = jnp.stack(mvp_l)
    new_swa_k_sample = jnp.stack(ks_l)
    new_swa_v_sample = jnp.stack(vs_l)
    new_hgrn_state_sample = jnp.stack(ss_l)
    return (yp, ys, new_swa_k_prompt, new_swa_v_prompt, new_hgrn_state_prompt, new_mem_k_prompt,
            new_mem_v_prompt, new_swa_k_sample, new_swa_v_sample, new_hgrn_state_sample)
```

```python
import math
from contextlib import ExitStack

import numpy as np
import ml_dtypes

import concourse.bass as bass
import concourse.mybir as mybir
from concourse.bass_utils import run_bass_kernel_spmd

F32 = mybir.dt.float32
BF16 = mybir.dt.bfloat16
AF = mybir.ActivationFunctionType
ALU = mybir.AluOpType

ALPHA = 2.0 ** 0.25
N_CORES = 8
D = 1024
MEM = 256

ENGS = ("pe", "act", "dve", "pool", "sp")
SAME_ENGINE_RAW = {"pe": False, "act": True, "dve": True, "pool": True, "sp": False}


class Buf:
    __slots__ = ("name", "w", "r")

    def __init__(self, name=""):
        self.name = name
        self.w = None
        self.r = {}


class Sched:
    def __init__(self, nc, n_dma_sems=40):
        self.nc = nc
        self.streams = {e: [] for e in ENGS}
        self.cnt = {e: 0 for e in ENGS}
        self.waited = {e: {} for e in ENGS}
        self.sems = {}
        self.n_dma = n_dma_sems
        self.dma_uses = [0] * n_dma_sems
        self.dma_next = 0
        self.n_sw = 0

    def _need(self, eng, tok, waits, same_ok):
        if tok is None:
            return
        key, val = tok
        if key == eng and not same_ok:
            return
        if self.waited[eng].get(key, 0) >= val:
            return
        if key in ENGS and key != eng and self.cnt[key] < val:
            raise RuntimeError(f"{eng} needs {key}>={val}, only {self.cnt[key]} incs emitted")
        if key == eng and self.cnt[key] < val:
            raise RuntimeError(f"{eng} self-wait on pending inc {val}")
        if waits.get(key, 0) < val:
            waits[key] = val

    @staticmethod
    def _flat(bufs):
        out = []
        for b in bufs:
            if isinstance(b, (list, tuple)):
                out.extend(Sched._flat(b))
            else:
                out.append(b)
        return out

    def _deps(self, eng, reads, writes):
        waits = {}
        se = SAME_ENGINE_RAW[eng]
        for b in reads:
            self._need(eng, b.w, waits, se)
        for b in writes:
            self._need(eng, b.w, waits, se)
            for k, v in b.r.items():
                self._need(eng, (k, v), waits, se)
        for k, v in waits.items():
            self.waited[eng][k] = v
        return list(waits.items())

    @staticmethod
    def _mark(tok, reads, writes):
        for b in writes:
            b.w = tok
            b.r = {}
        for b in reads:
            if b.r.get(tok[0], 0) < tok[1]:
                b.r[tok[0]] = tok[1]

    def op(self, eng, fn, reads=(), writes=(), inc=True):
        reads, writes = self._flat(reads), self._flat(writes)
        waits = self._deps(eng, reads, writes)
        if inc:
            self.cnt[eng] += 1
            tok = (eng, self.cnt[eng])
        else:
            tok = (eng, self.cnt[eng] + 1)
        self.streams[eng].append((waits, fn, (eng, 1) if inc else None))
        self._mark(tok, reads, writes)
        return tok

    def dma(self, q, out, in_, reads=(), writes=()):
        reads, writes = self._flat(reads), self._flat(writes)
        waits = self._deps(q, reads, writes)
        if q == "pool":
            key = ("sw", self.n_sw)
            self.n_sw += 1
            tok = (key, 16)
        else:
            idx = self.dma_next
            self.dma_next = (self.dma_next + 1) % self.n_dma
            key = ("dma", idx)
            prev = self.dma_uses[idx] * 16
            if prev and self.waited[q].get(key, 0) < prev:
                waits.append((key, prev))
                self.waited[q][key] = prev
            self.dma_uses[idx] += 1
            tok = (key, self.dma_uses[idx] * 16)

        def fn(e, out=out, in_=in_):
            return e.dma_start(out=out, in_=in_)
        self.streams[q].append((waits, fn, (key, 16)))
        self._mark(tok, reads, writes)
        return tok

    def inherit(self, dst, srcs):
        for s in srcs:
            if s.w is not None and dst.r.get(s.w[0], 0) < s.w[1]:
                dst.r[s.w[0]] = s.w[1]
            for k, v in s.r.items():
                if dst.r.get(k, 0) < v:
                    dst.r[k] = v

    def wait_all(self, eng, toks):
        waits = {}
        for t in toks:
            self._need(eng, t, waits, True)
        for k, v in waits.items():
            self.waited[eng][k] = v
        self.streams[eng].append((list(waits.items()), None, None))

    def emit(self, stack):
        nc = self.nc
        for e in ("pe", "act", "dve", "pool"):
            self.sems[e] = stack.enter_context(nc.semaphore("s_" + e))
        for i in range(self.n_dma):
            self.sems[("dma", i)] = stack.enter_context(nc.semaphore(f"s_dma{i}"))
        for i in range(self.n_sw):
            self.sems[("sw", i)] = stack.enter_context(nc.semaphore(f"s_sw{i}"))
        block = stack.enter_context(nc.Block())
        sems = self.sems

        def replay(stream):
            def run(eng):
                for waits, fn, inc in stream:
                    for k, v in waits:
                        eng.wait_ge(sems[k], v)
                    if fn is None:
                        continue
                    ins = fn(eng)
                    if inc is not None:
                        ins.then_inc(sems[inc[0]], inc[1])
            return run

        block.tensor(replay(self.streams["pe"]))
        block.scalar(replay(self.streams["act"]))
        block.vector(replay(self.streams["dve"]))
        block.gpsimd(replay(self.streams["pool"]))
        block.sync(replay(self.streams["sp"]))


class Ring:
    def __init__(self, items):
        self.items = items
        self.i = 0

    def next(self):
        it = self.items[self.i]
        self.i = (self.i + 1) % len(self.items)
        return it


WI_HF, WI_HQ, WI_HG, WI_AQ, WI_AKD, WI_HI, WI_AVK = range(7)
WO0 = 7
WQ0 = 9
WMO0 = 11
WKV0 = 13
WF0 = 17
WFO0 = 28
NBLK = 34
FFO_GROUPS = ((0, 8), (8, 16), (16, 22))


def _consts():
    slopes = 2.0 ** (-8.0 * np.arange(1, 9) / 8.0)
    k = np.arange(128)[:, None, None]
    q = np.arange(128)[None, None, :]
    sl = slopes[None, :, None]
    kc, qc = k // 64, q // 64
    ep_prev = np.exp(-sl * (128 + q - k)) * ~((kc == 0) & (qc == 1))
    ep_cur = np.exp(-sl * np.abs(q - k)) * ~((kc == 1) & (qc == 0))
    qs = np.arange(16)[None, None, :]
    es_prev = np.exp(-sl * (128 + qs - k))
    ks = np.arange(16)[:, None, None]
    es_cur = np.zeros((128, 8, 16))
    es_cur[:16] = np.exp(-sl * np.abs(qs - ks))
    s = np.arange(128)[:, None]
    t = np.arange(128)[None, :]
    mask_p = ((s // 64 == t // 64) & (s <= t)).astype(np.float32)
    mask_s = np.zeros((128, 128), np.float32)
    mask_s[:16, :16] = (s[:16] <= t[:, :16])
    smask_p = (np.arange(512) % 64 == 0).astype(np.float32)
    smask_s = (np.arange(512) == 0).astype(np.float32)
    bf = ml_dtypes.bfloat16
    return {
        "c_identf": np.eye(128, dtype=np.float32),
        "c_identb": np.eye(128, dtype=np.float32).astype(bf),
        "c_maskp": mask_p,
        "c_epp": ep_prev.reshape(128, 1024).astype(bf),
        "c_epc": ep_cur.reshape(128, 1024).astype(bf),
    }


class _Stop(Exception):
    pass


DEBUG_STOP = [None]


def build(NB, SEQ, sample=True):
    def chk(n):
        if DEBUG_STOP[0] == n:
            raise _Stop()
    nc = bass.Bass("TRN2", target_bir_lowering=False)
    TTP = 512
    NTILE = SEQ // TTP
    assert SEQ % TTP == 0

    def din(name, shape, dt=F32):
        return nc.dram_tensor(name, list(shape), dt, kind="ExternalInput").ap()

    def dout(name, shape):
        return nc.dram_tensor(name, list(shape), F32, kind="ExternalOutput").ap()

    xp = din("xp", [NB * SEQ, D])
    memp = din("memp", [NB * MEM, D])
    xs = din("xs", [16, D])
    ck = din("ck", [128, 128])
    cv = din("cv", [128, 128])
    sh = din("sh", [4, 128, 128])
    cmk = din("cmk", [MEM, D])
    cmv = din("cmv", [MEM, D])
    w_in = din("w_in", [D, 2816])
    lbl = din("lbl", [2, 512])
    gn = din("gn", [1, 128])
    sinks = din("sinks", [1, 8])
    w_out = din("w_out", [D, D])
    w_mq = din("w_mq", [D, D])
    w_mkv = din("w_mkv", [D, 2 * D])
    w_mo = din("w_mo", [D, D])
    w_fi = din("w_fi", [D, 5632])
    w_fo = din("w_fo", [2816, D])
    ln_g = din("ln_g", [3, D])
    ln_b = din("ln_b", [3, D])
    c_identf = din("c_identf", [128, 128])
    c_identb = din("c_identb", [128, 128], BF16)
    c_maskp = din("c_maskp", [128, 128])
    c_epp = din("c_epp", [128, 1024], BF16)
    c_epc = din("c_epc", [128, 1024], BF16)

    yp = dout("yp", [NB * SEQ, D])
    ys = dout("ys", [16, D])
    kp = dout("kp", [NB, 128, 128])
    vp = dout("vp", [NB, 128, 128])
    sp_o = dout("sp_o", [NB, 4, 128, 128])
    mkp = dout("mkp", [NB * MEM, D])
    mvp = dout("mvp", [NB * MEM, D])
    ks_o = dout("ks_o", [16, 128])
    vs_o = dout("vs_o", [16, 128])
    ss_o = dout("ss_o", [4, 128, 128])

    ws = nc.dram_tensor("ws", [NBLK, 128, 4096], BF16, kind="Internal").ap()

    st = ExitStack()
    S = Sched(nc)
    out_toks = []

    def sb(name, shape, dt):
        return st.enter_context(nc.sbuf_tensor(name, list(shape), dt))

    identf = sb("identf", [128, 128], F32)
    identb = sb("identb", [128, 128], BF16)
    maskp = sb("maskp", [128, 128], F32)
    epp = sb("epp", [128, 8, 128], BF16)
    epc = sb("epc", [128, 8, 128], BF16)
    fst = sb("fst", [128, 512], F32)
    grep_ = sb("grep", [128, 3, D], F32)
    brep = sb("brep", [128, 3, D], F32)
    small = sb("small", [128, 64], F32)
    onescol = sb("onescol", [128, 2], BF16)
    epsb = sb("epsb", [128, 2], F32)
    gbT = sb("gbT", [128, 48], F32)
    C = Buf("consts")
    fstb = Buf("fst")
    gch = [0]

    NW = 3
    wring_t = sb("wring", [128, NW, 4096], BF16)
    wring = Ring([(wring_t[:, i, :], Buf(f"w{i}")) for i in range(NW)])
    xpool_t = sb("xpool", [128, 2, D], F32)
    xhb_ = [Buf(f"xh{i}") for i in range(4)]
    xpool = Ring([(xpool_t[:, i, :], [xhb_[2 * i], xhb_[2 * i + 1]]) for i in range(2)])
    actT_t = sb("actT", [128, 2, 8, 512], BF16)
    actT_f = (actT_t[:, 0], Buf("aTf"))
    actT_b = (actT_t[:, 1], [Buf(f"aTb{k_}") for k_ in range(8)])
    xrl_t = actT_t[:, 1].rearrange("p k t -> p (k t)").bitcast(F32).rearrange("p (j c) -> p j c", j=4)
    xhalf = Ring([(xrl_t[:, j, :], [actT_b[1][2 * j], actT_b[1][2 * j + 1]]) for j in range(4)])

    ARENA_B = 45056 + 24576
    arena = sb("arena", [128, ARENA_B // 4], F32)

    def carve(off, nbytes, dt, pat=None, **kw):
        a = arena[:, off // 4:(off + nbytes) // 4]
        if dt == BF16:
            a = a.bitcast(BF16)
        if pat:
            a = a.rearrange(pat, **kw)
        return a
    KB = 1024
    eG = carve(0, 8 * KB, F32, "p (h t) -> p h t", h=4)
    qgT = carve(8 * KB, 4 * KB, BF16, "p (h t) -> p h t", h=4)
    kgT = carve(12 * KB, 4 * KB, BF16, "p (h t) -> p h t", h=4)
    kdecT = carve(16 * KB, 4 * KB, BF16, "p (h t) -> p h t", h=4)
    shg = carve(20 * KB, 4 * KB, BF16, "p (h t) -> p h t", h=4)
    QT = carve(24 * KB, 4 * KB, BF16, "p (h t) -> p h t", h=4)
    vtok = carve(28 * KB, 4 * KB, BF16, "p (c f) -> p c f", c=4)
    gtmp_t = carve(32 * KB, 8 * KB, F32, "p (n t) -> p n t", n=4)
    PT = carve(40 * KB, 4 * KB, BF16, "p (k h q) -> p k h q", k=2, h=8)
    CD0 = 44 * KB
    qmT = carve(CD0, 8 * KB, BF16, "p (c t) -> p c t", c=8)
    PTm = carve(CD0 + 8 * KB, 8 * KB, BF16, "p (m h t) -> p m h t", m=2, h=4)
    omT = carve(CD0 + 16 * KB, 8 * KB, BF16, "p (c t) -> p c t", c=8)
    hT = carve(CD0, 22 * KB, BF16, "p (c t) -> p c t", c=22)
    stg3 = carve(CD0, 16 * KB, F32, "p (c f) -> p c f", c=4)
    stg3b = [Buf(f"stg3_{i}") for i in range(4)]
    A_bufs = {n: Buf(n) for n in ["eG", "qgT", "kgT", "kdecT", "shg", "QT", "vtok", "PT", "g0", "g1", "g2", "g3"]}
    CD_bufs = {n: Buf(n) for n in ["qmT", "PTm", "omT", "hT"]}
    gtmp = Ring([(gtmp_t[:, i, :], A_bufs[f"g{i}"]) for i in range(4)])

    KT = sb("KT", [128, 2, 128 + 512], BF16)
    KTb = Buf("KT")
    Vaug = sb("Vaug", [128, 5, 2, 65], BF16)
    Vaugb = [Buf(f"va{i}") for i in range(5)]
    AT_t = sb("AT", [128, 2, 4, 128], BF16)
    ATr = Ring([(AT_t[:, i], Buf(f"AT{i}")) for i in range(2)])
    kdt_t = sb("kdt", [128, 2, 4, 128], BF16)
    kdtr = Ring([(kdt_t[:, i], Buf(f"kdt{i}")) for i in range(2)])
    Sst = sb("Sst", [128, 4, 128], F32)
    Sb = [Buf(f"S{h_}") for h_ in range(4)]
    Sbf_t = sb("Sbf", [128, 3, 4, 128], BF16)
    Sbf = [(Sbf_t[:, i], Buf(f"Sbf{i}")) for i in range(3)]
    on_t = sb("on", [128, 2, 4, 128], BF16)
    onr = Ring([(on_t[:, i], Buf(f"on{i}")) for i in range(2)])
    pt_t = sb("pt", [128, 2, 4, 128], BF16)
    ptr_ = Ring([(pt_t[:, i], Buf(f"pt{i}")) for i in range(2)])
    oa_t = sb("oa", [128, 2, 512], BF16)
    oar = Ring([(oa_t[:, i], Buf(f"oa{i}")) for i in range(2)])
    junk = sb("junk", [128, 512], F32)
    junkb = Buf("junk")
    lnst_t = sb("lnst", [128, 2, 4, 16], F32)
    lnring = Ring([(lnst_t[:, i], Buf(f"lnst{i}")) for i in range(2)])
    lncur = [lnring.next()]

    def half_stats(TC, tc, half):
        lnst, lnstb = lncur[0]
        dve(lambda e: e.bn_stats(out=lnst[0:TC, tc, half * 6:(half + 1) * 6], in_=resid[0:TC, tc, half * 512:(half + 1) * 512]),
            [residb[tc]], [lnstb])
    stat = sb("stat", [128, 8, 16], F32)
    statr = Ring([(stat[:, i, :], Buf(f"stat{i}")) for i in range(8)])
    mixT = sb("mixT", [128, 8, 512], BF16)
    mixTb = [Buf("mixT_h"), Buf("mixT_a")]
    resid = sb("resid", [128, 4, D], F32)
    residb = [Buf(f"res{i}") for i in range(4)]
    mkT = sb("mkT", [128, 8, 256], BF16)
    mkTb = Buf("mkT")
    mvb = sb("mvb", [128, 2, D], BF16)
    mvbb = Buf("mvb")
    memT = carve(32 * KB, 4 * KB, BF16, "p (c m) -> p c m", c=8)
    memTbs = [A_bufs["g0"], A_bufs["g1"]]
    stgr = Ring([(gtmp_t[:, 2, :], A_bufs["g2"]), (gtmp_t[:, 3, :], A_bufs["g3"])])
    om_t = sb("om", [128, 2, D], BF16)
    omr = Ring([(om_t[:, i, :], Buf(f"om{i}")) for i in range(2)])
    sg_t = sb("sg", [128, 2, 512], BF16)
    sgr = Ring([(sg_t[:, i, :], Buf(f"sg{i}")) for i in range(2)])

    banks_t = [st.enter_context(nc.psum_tensor(f"bank{i}", [128, 512], F32)) for i in range(8)]
    bank_list = [(banks_t[i], Buf(f"bank{i}")) for i in range(8)]
    bank_held = {}
    bank_ptr = [0]

    def nb(hold=None):
        for _ in range(9):
            i = bank_ptr[0]
            bank_ptr[0] = (i + 1) % 8
            if i not in bank_held.values():
                if hold is not None:
                    bank_held[hold] = i
                t, b = bank_list[i]
                return t[:], b
        raise RuntimeError("no free PSUM bank")

    def nb_release(key):
        bank_held.pop(key, None)

    def mm(out, lhsT, rhs, start, stop, reads, writes, inc):
        S.op("pe", lambda e: e.matmul(out, lhsT=lhsT, rhs=rhs, start=start, stop=stop), reads, writes, inc)

    def tr(out, in_, ident, reads, writes, inc):
        S.op("pe", lambda e: e.transpose(out, in_, ident), reads, writes, inc)

    def act(out, in_, func, reads, writes, **kw):
        S.op("act", lambda e: e.activation(out=out, in_=in_, func=func, **kw), reads, writes)

    def tt(out, in0, in1, op, reads, writes):
        S.op("dve", lambda e: e.tensor_tensor(out=out, in0=in0, in1=in1, op=op), reads, writes)

    def ptt(out, in0, in1, op, reads, writes):
        S.op("pool", lambda e: e.tensor_tensor(out=out, in0=in0, in1=in1, op=op), reads, writes)

    def ts(out, in0, s1, s2, op0, op1, reads, writes):
        S.op("dve", lambda e: e.tensor_scalar(out=out, in0=in0, scalar1=s1, scalar2=s2, op0=op0, op1=op1), reads, writes)

    def stt(out, in0, scalar, in1, op0, op1, reads, writes):
        S.op("dve", lambda e: e.scalar_tensor_tensor(out=out, in0=in0, scalar=scalar, in1=in1, op0=op0, op1=op1), reads, writes)

    def dve(fn, reads, writes):
        S.op("dve", fn, reads, writes)

    for dst, src in ((identf, c_identf), (identb, c_identb), (maskp, c_maskp),
                     (epp, c_epp.rearrange("p (h q) -> p h q", h=8)), (epc, c_epc.rearrange("p (h q) -> p h q", h=8))):
        S.dma("sp", dst[:], src, writes=[C])
    for i in range(3):
        S.dma("sp", grep_[:, i, :], ln_g[i:i + 1, :].partition_broadcast(128), writes=[C])
        S.dma("sp", brep[:, i, :], ln_b[i:i + 1, :].partition_broadcast(128), writes=[C])
    for li_ in range(3):
        for kc_ in range(8):
            S.dma("sp", gbT[:, li_ * 8 + kc_:li_ * 8 + kc_ + 1], ln_g[li_:li_ + 1, kc_ * 128:(kc_ + 1) * 128].rearrange("o p -> p o"), writes=[C])
            S.dma("sp", gbT[:, 24 + li_ * 8 + kc_:24 + li_ * 8 + kc_ + 1], ln_b[li_:li_ + 1, kc_ * 128:(kc_ + 1) * 128].rearrange("o p -> p o"), writes=[C])
    for l_ in range(2):
        for h_ in range(4):
            S.dma("sp", small[:, l_ * 4 + h_:l_ * 4 + h_ + 1], lbl[l_:l_ + 1, h_ * 128:(h_ + 1) * 128].rearrange("o p -> p o"), writes=[C])
    S.dma("sp", small[:, 16:17], gn.rearrange("o p -> p o"), writes=[C])
    S.dma("sp", small[:, 24:32], sinks.partition_broadcast(128), writes=[C])
    dve(lambda e: e.memset(fst[:], 0.0), [], [fstb])
    dve(lambda e: e.memset(onescol[:], 1.0), [], [C])
    dve(lambda e: e.memset(Vaug[:], 1.0), [], Vaugb)
    dve(lambda e: e.memset(epsb[:, 0:1], 1e-5), [], [C])
    dve(lambda e: e.memset(epsb[:, 1:2], 1e-6), [], [C])
    tt(small[:, 32:36], small[:, 4:8], small[:, 0:4], ALU.subtract, [C], [C])
    act(small[:, 32:36], small[:, 32:36], AF.Exp, [C], [C])
    ts(small[:, 32:36], small[:, 32:36], 1.0, None, ALU.add, ALU.bypass, [C], [C])
    dve(lambda e: e.reciprocal(out=small[:, 36:40], in_=small[:, 32:36]), [C], [C])
    ts(small[:, 8:12], small[:, 36:40], 0.5, 0.5, ALU.mult, ALU.add, [C], [C])
    ts(small[:, 12:16], small[:, 36:40], -0.5, 0.5, ALU.mult, ALU.add, [C], [C])
    act(small[:, 24:32], small[:, 24:32], AF.Exp, [C], [C])
    ca, cb, gnorm, esink = small[:, 8:12], small[:, 12:16], small[:, 16:17], small[:, 24:32]

    wsb = [Buf(f"ws{i}") for i in range(NBLK)]

    def cast(blk, k0, k1, c0, c1, src, s0, s1, ncols):
        nk = k1 - k0
        dst = ws[blk, :, 0:nk * ncols].rearrange("p (k c) -> p k c", c=ncols)[:, :, c0:c1]
        S.dma("pool", dst, src.rearrange("(k p) c -> p k c", p=128)[:, k0:k1, s0:s1], writes=[wsb[blk]])

    cast(WI_HF, 0, 8, 0, 512, w_in, 512, 1024, 512)
    cast(WI_HQ, 0, 8, 0, 512, w_in, 0, 512, 512)
    cast(WI_HG, 0, 8, 0, 512, w_in, 1536, 2048, 512)
    cast(WI_AQ, 0, 8, 0, 512, w_in, 2048, 2560, 512)
    for j in range(4):
        kvh = j // 2
        cast(WI_AKD, 0, 8, j * 64, (j + 1) * 64, w_in, 2560 + kvh * 64, 2560 + (kvh + 1) * 64, 256)
    cast(WI_HI, 0, 8, 0, 512, w_in, 1024, 1536, 512)
    cast(WI_AVK, 0, 8, 0, 128, w_in, 2688, 2816, 256)
    cast(WI_AVK, 0, 8, 128, 256, w_in, 2560, 2688, 256)
    for h in range(2):
        cast(WO0 + h, 0, 8, 0, 512, w_out, h * 512, (h + 1) * 512, 512)
        cast(WQ0 + h, 0, 8, 0, 512, w_mq, h * 512, (h + 1) * 512, 512)
        cast(WMO0 + h, 0, 8, 0, 512, w_mo, h * 512, (h + 1) * 512, 512)
    for j in range(4):
        cast(WKV0 + j, 0, 8, 0, 512, w_mkv, j * 512, (j + 1) * 512, 512)
    for j in range(11):
        cast(WF0 + j, 0, 8, 0, 256, w_fi, 256 * j, 256 * j + 256, 512)
        cast(WF0 + j, 0, 8, 256, 512, w_fi, 2816 + 256 * j, 2816 + 256 * j + 256, 512)
    for h in range(2):
        for gi, (k0, k1) in enumerate(FFO_GROUPS):
            cast(WFO0 + h * 3 + gi, k0, k1, 0, 512, w_fo, h * 512, (h + 1) * 512, 512)

    w_held = {}
    w_ptr = [0]

    def wload(blk, nk, ncols, who="B"):
        w_held.pop(who, None)
        for _ in range(NW + 1):
            i = w_ptr[0]
            w_ptr[0] = (i + 1) % NW
            if i not in w_held.values():
                break
        else:
            raise RuntimeError("no free weight slot")
        w_held[who] = i
        wt, wb = wring.items[i]
        n = nk * ncols
        S.dma("sp", wt[:, 0:n], ws[blk, :, 0:n], reads=[wsb[blk]], writes=[wb])
        return wt[:, 0:n].rearrange("p (k c) -> p k c", c=ncols), wb

    def transpose_rows_f32(src, srcb, TC, dstT, dstb, tcol0):
        for half in range(2):
            bk, bb = nb()
            for j in range(4):
                kc = half * 4 + j
                tr(bk[:, j * TC:(j + 1) * TC], src[0:TC, kc * 128:(kc + 1) * 128], identf[0:TC, 0:TC],
                   [srcb, C], [bb], j == 3)
            act(dstT[:, half * 4:half * 4 + 4, tcol0:tcol0 + TC],
                bk[:, 0:4 * TC].rearrange("p (j t) -> p j t", t=TC), AF.Copy, [bb], dstb if isinstance(dstb, list) else [dstb])

    def layer_norm(tcs, TC, li, aTout, aTb, deferred):
        n_ = len(tcs)
        lnst, lnstb = lncur[0]
        lncur[0] = lnring.next()
        for tc in tcs:
            dve(lambda e, tc=tc: e.bn_aggr(out=lnst[0:TC, tc, 12:14], in_=lnst[0:TC, tc, 0:12]), [lnstb], [lnstb])
        yield
        act(lnst[0:TC, 0:n_, 14:15], lnst[0:TC, 0:n_, 13:14], AF.Ln, [lnstb, C], [lnstb], bias=epsb[0:TC, 0:1])
        act(lnst[0:TC, 0:n_, 14:15], lnst[0:TC, 0:n_, 14:15], AF.Exp, [lnstb], [lnstb], scale=-0.5)
        stt(lnst[0:TC, 0:n_, 15:16], lnst[0:TC, 0:n_, 12:13], -1.0, lnst[0:TC, 0:n_, 14:15], ALU.mult, ALU.mult, [lnstb], [lnstb])
        for tc in tcs:
            r = resid[0:TC, tc, :]
            rb = residb[tc]
            act(r, r, AF.Identity, [rb, lnstb], [rb], scale=lnst[0:TC, tc, 14:15], bias=lnst[0:TC, tc, 15:16])
            yield
        if aTout is not None:
            for kc in range(8):
                yield
                bk, bb = nb()
                for tc in tcs:
                    tr(bk[:, tc * TC:(tc + 1) * TC], resid[0:TC, tc, kc * 128:(kc + 1) * 128], identf[0:TC, 0:TC],
                       [residb[tc], C], [bb], tc == tcs[-1])
                gcol = gbT[:, li * 8 + kc:li * 8 + kc + 1]
                bcol = gbT[:, 24 + li * 8 + kc:24 + li * 8 + kc + 1]
                ncol = len(tcs) * TC
                if kc % 2 == 0:
                    act(aTout[:, kc, 0:ncol], bk[:, 0:ncol], AF.Identity, [bb, C], [aTb[kc]], scale=gcol, bias=bcol)
                else:
                    ts(aTout[:, kc, 0:ncol], bk[:, 0:ncol], gcol, bcol, ALU.mult, ALU.add, [bb, C], [aTb[kc]])
        for tc in tcs:
            def affine(tc=tc):
                r = resid[0:TC, tc, :]
                rb = residb[tc]
                ptt(r, r, grep_[0:TC, li, :], ALU.mult, [rb, C], [rb])
                ptt(r, r, brep[0:TC, li, :], ALU.add, [rb, C], [rb])
            if deferred is None:
                S.inherit(stg3b[tc], [CD_bufs["hT"]])
                tt(stg3[0:TC, tc, :], resid[0:TC, tc, :], grep_[0:TC, li, :], ALU.mult, [residb[tc], C], [stg3b[tc]])
                ptt(stg3[0:TC, tc, :], stg3[0:TC, tc, :], brep[0:TC, li, :], ALU.add, [stg3b[tc], C], [stg3b[tc]])
                yield
            else:
                deferred.append(affine)

    def out_proj_residual(tcs, TC, lhsT_of, lhs_bufs, wblk0, res_in):
        for half in range(2):
            wv, wb = wload(wblk0 + half, 8, 512)
            for tc in tcs:
                bk, bb = nb()
                for kc in range(8):
                    mm(bk[0:TC, :], lhsT_of(kc, tc), wv[:, kc, :], kc == 0, kc == 7, lhs_bufs + [wb], [bb], kc == 7)
                rin, rinb = res_in(tc, half)
                stt(resid[0:TC, tc, half * 512:(half + 1) * 512], rin, ALPHA, bk[0:TC, :],
                    ALU.mult, ALU.add, [rinb, bb], [residb[tc]])
                half_stats(TC, tc, half)
                yield

    def mem_prepare(src_k, src_v, kout, vout, compute):
        if compute:
            for mt in range(2):
                xt, xtb = xpool.next()
                S.dma("sp", xt[:, :], src_k[mt * 128:(mt + 1) * 128, :], writes=[xtb])
                transpose_rows_f32(xt, xtb, 128, memT, memTbs, mt * 128)
            for j in range(4):
                wv, wb = wload(WKV0 + j, 8, 512)
                for mt in range(2):
                    bk, bb = nb()
                    for kc in range(8):
                        mm(bk, memT[:, kc, mt * 128:(mt + 1) * 128], wv[:, kc, :], kc == 0, kc == 7, memTbs + [wb], [bb], kc == 7)
                    sg_, sgb = stgr.next()
                    act(sg_, bk, AF.Copy, [bb], [sgb])
                    dst = kout if j < 2 else vout
                    out_toks.append(S.dma("act", dst[mt * 128:(mt + 1) * 128, (j % 2) * 512:(j % 2 + 1) * 512], sg_, reads=[sgb]))
                    if j < 2:
                        b2, b2b = nb()
                        for i in range(4):
                            tr(b2[:, i * 128:(i + 1) * 128], sg_[:, i * 128:(i + 1) * 128], identf[:], [sgb, C], [b2b], i == 3)
                        act(mkT[:, j * 4:(j + 1) * 4, mt * 128:(mt + 1) * 128], b2.rearrange("p (i m) -> p i m", i=4), AF.Copy, [b2b], [mkTb])
                    else:
                        dve(lambda e, mt=mt, j=j, sg_=sg_: e.tensor_copy(out=mvb[:, mt, (j - 2) * 512:(j - 1) * 512], in_=sg_), [sgb], [mvbb])
        else:
            for mt in range(2):
                xt, xtb = xpool.next()
                S.dma("sp", xt[:, :], src_k[mt * 128:(mt + 1) * 128, :], writes=[xtb])
                transpose_rows_f32(xt, xtb, 128, mkT, mkTb, mt * 128)
                xt2, xt2b = xpool.next()
                S.dma("sp", xt2[:, :], src_v[mt * 128:(mt + 1) * 128, :], writes=[xt2b])
                act(mvb[:, mt, :], xt2[:, :], AF.Copy, [xt2b], [mvbb])

    def emit_tile(TT, TC, L, x_rows, y_rows, first, last, is_sample, kv_out):
        NTC = TT // TC
        CP = TC // L
        NCH = TT // L
        tcs = list(range(NTC))
        maskA = maskp
        Eprev = epp
        Ecur = epc
        NKP = 128

        def tsl(tc):
            return slice(tc * TC, (tc + 1) * TC)


        xT, xTb = actT_f
        for tc in tcs:
            xt, xtb = xpool.next()
            S.dma("sp", xt[0:TC, :], x_rows(tc), writes=[xtb])
            transpose_rows_f32(xt, xtb, TC, xT, xTb, tc * TC)
            yield

        def fm_proj(blk, nchunks, ncols, consume):
            wv, wb = wload(blk, 8, ncols, "F")
            for j in range(nchunks):
                bk, bb = nb()
                for kc in range(8):
                    mm(bk[:, 0:TT], wv[:, kc, j * 128:(j + 1) * 128], xT[:, kc, 0:TT], kc == 0, kc == 7, [wb, xTb], [bb], kc == 7)
                r_ = consume(j, bk[:, 0:TT], bb)
                if r_ is not None:
                    yield from r_
                yield

        def c_hf(h, z, zb):
            t1, t1b = gtmp.next()
            t1 = t1[:, 0:TT]
            act(t1, z, AF.Tanh, [zb], [t1b], scale=0.5)
            ts(t1, t1, cb[:, h:h + 1], ca[:, h:h + 1], ALU.mult, ALU.add, [t1b, C], [t1b])
            f3 = t1.rearrange("p (c l) -> p c l", l=L)
            dve(lambda e: e.tensor_copy(out=fst[:, 0:TT].rearrange("p (c l) -> p c l", l=L)[:, :, 0:1], in_=f3[:, :, 0:1]), [t1b], [fstb])
            eg = eG[:, h, 0:TT]
            dve(lambda e: e.tensor_tensor_scan(out=eg, data0=t1, data1=fst[:, 0:TT], initial=1.0, op0=ALU.mult, op1=ALU.max),
                [t1b, fstb], [A_bufs["eG"]])
            t2, t2b = gtmp.next()
            t2 = t2[:, 0:TT]
            yield
            dve(lambda e: e.reciprocal(out=t2, in_=eg), [A_bufs["eG"]], [t2b])
            yield
            t3, t3b = gtmp.next()
            t3 = t3[:, 0:TT]
            stt(t3, t1, -1.0, t2, ALU.add, ALU.mult, [t1b, t2b], [t3b])
            act(kgT[:, h, 0:TT], t3, AF.Copy, [t3b], [A_bufs["kgT"]], scale=-1.0)
            egl = eg.rearrange("p (c l) -> p c l", l=L)[:, :, L - 1:L].to_broadcast([128, NCH, L])
            stt(kdecT[:, h, 0:TT].rearrange("p (c l) -> p c l", l=L), t3.rearrange("p (c l) -> p c l", l=L), -1.0, egl,
                ALU.mult, ALU.mult, [t3b, A_bufs["eG"]], [A_bufs["kdecT"]])

        def c_hq(h, z, zb):
            t1, t1b = gtmp.next()
            t1 = t1[:, 0:TT]
            act(t1, z, AF.Silu, [zb], [t1b])
            ptt(qgT[:, h, 0:TT], t1, eG[:, h, 0:TT], ALU.mult, [t1b, A_bufs["eG"]], [A_bufs["qgT"]])

        def c_hg(h, z, zb):
            act(shg[:, h, 0:TT], z, AF.Silu, [zb], [A_bufs["shg"]])

        def c_aq(j, z, zb):
            act(QT[:, j, 0:TT], z, AF.Copy, [zb], [A_bufs["QT"]], scale=0.125)

        def c_akd(j, z, zb):
            act(KT[:, j, NKP:NKP + TT], z, AF.Copy, [zb], [KTb])

        yield from fm_proj(WI_HG, 4, 512, c_hg)
        yield from fm_proj(WI_AQ, 4, 512, c_aq)
        yield from fm_proj(WI_AKD, 2, 256, c_akd)
        yield from fm_proj(WI_HF, 4, 512, c_hf)
        yield from fm_proj(WI_HQ, 4, 512, c_hq)

        wv, wb = wload(WI_HI, 8, 512, "F")
        for tc in tcs:
            bk, bb = nb()
            for kc in range(8):
                mm(bk[0:TC, :], xT[:, kc, tsl(tc)], wv[:, kc, :], kc == 0, kc == 7, [xTb, wb], [bb], kc == 7)
            act(vtok[0:TC, tc, :], bk[0:TC, :], AF.Copy, [bb], [A_bufs["vtok"]])
            yield
        wv, wb = wload(WI_AVK, 8, 256, "F")
        for tc in tcs:
            bk, bb = nb()
            for kc in range(8):
                mm(bk[0:TC, 0:256], xT[:, kc, tsl(tc)], wv[:, kc, :], kc == 0, kc == 7, [xTb, wb], [bb], kc == 7)
            act(Vaug[0:TC, tc + 1, :, 0:64], bk[0:TC, 0:128].rearrange("p (k d) -> p k d", k=2), AF.Copy, [bb], [Vaugb[tc + 1]])
            if kv_out is not None and tc == NTC - 1:
                kvo_t, kvob = gtmp.next()
                act(kvo_t[0:TC, 0:256], bk[0:TC, 0:256], AF.Copy, [bb], [kvob])
                out_toks.append(S.dma("act", kv_out[1], kvo_t[0:TC, 0:128], reads=[kvob]))
                out_toks.append(S.dma("act", kv_out[0], kvo_t[0:TC, 128:256], reads=[kvob]))
            yield

        for tc in tcs:
            bk, bb = nb()
            for h in range(4):
                mm(bk[0:TC, h * TC:(h + 1) * TC], kgT[:, h, tsl(tc)], qgT[:, h, tsl(tc)], True, True,
                   [A_bufs["kgT"], A_bufs["qgT"]], [bb], h == 3)
            AT, ATb = ATr.next()
            tt(AT[0:TC, :, 0:TC], bk[0:TC, 0:4 * TC].rearrange("p (h t) -> p h t", h=4),
               maskA[0:TC, 0:TC].unsqueeze(1).to_broadcast([TC, 4, TC]), ALU.mult, [bb, C], [ATb])
            bk, bb = nb()
            bkb = bk.bitcast(BF16)
            for h in range(4):
                tr(bkb[0:TC, h * 128:(h + 1) * 128], kdecT[:, h, tsl(tc)], identb[:], [A_bufs["kdecT"], C], [bb], h == 3)
            kdt, kdtb = kdtr.next()
            act(kdt[0:TC], bkb[0:TC, 0:512].rearrange("p (h d) -> p h d", h=4), AF.Copy, [bb], [kdtb])

            use_prev = (not first) or tc > 0
            kts = ([("prev", NKP, tc * TC if not is_sample else 0)] if use_prev else []) + [("cur", TC, NKP + tc * TC)]
            for kti, (kname, nk, kcol) in enumerate(kts):
                E = Eprev if kname == "prev" else Ecur
                for rt in range(2):
                    bk, bb = nb()
                    for kvh in range(2):
                        for gg in range(2):
                            idx = kvh * 2 + gg
                            mm(bk[0:nk, idx * TC:(idx + 1) * TC], KT[rt * 64:(rt + 1) * 64, kvh, kcol:kcol + nk],
                               QT[rt * 64:(rt + 1) * 64, kvh * 2 + gg, tsl(tc)], True, True, [KTb, A_bufs["QT"]], [bb], idx == 3)
                    pt, ptb = ptr_.next()
                    act(pt[0:nk, :, 0:TC], bk[0:nk, 0:4 * TC].rearrange("p (g q) -> p g q", g=4), AF.Exp, [bb], [ptb])
                    tt(PT[0:nk, kti, rt:8:2, 0:TC], pt[0:nk, :, 0:TC], E[0:nk, rt:8:2, 0:TC], ALU.mult,
                       [ptb, C], [A_bufs["PT"]])
            yield
            obk, obb = nb(hold="F_o")
            for ci in range(CP):
                c = tc * CP + ci
                rows = slice(ci * L, (ci + 1) * L)
                toks = slice(tc * TC + ci * L, tc * TC + (ci + 1) * L)
                sbf, sbfb = Sbf[gch[0] % 3]
                for h in range(4):
                    o_out = obk[rows, h * 128:(h + 1) * 128]
                    mm(o_out, qgT[:, h, toks], sbf[:, h, :], True, False, [A_bufs["qgT"], sbfb], [obb], False)
                    mm(o_out, AT[rows, h, ci * L:(ci + 1) * L], vtok[rows, tc, h * 128:(h + 1) * 128], False, True,
                       [ATb, A_bufs["vtok"]], [obb], h == 3 and ci == CP - 1)
                ubk, ubb = nb()
                for h in range(4):
                    mm(ubk[:, h * 128:(h + 1) * 128], kdt[rows, h, :], vtok[rows, tc, h * 128:(h + 1) * 128], True, True,
                       [kdtb, A_bufs["vtok"]], [ubb], h == 3)
                for h in range(4):
                    col = c * L + L - 1
                    stt(Sst[:, h, :], Sst[:, h, :], eG[:, h, col:col + 1], ubk[:, h * 128:(h + 1) * 128], ALU.mult, ALU.add,
                        [Sb[h], A_bufs["eG"], ubb], [Sb[h]])
                gch[0] += 1
                nsbf, nsbfb = Sbf[gch[0] % 3]
                act(nsbf, Sst[:], AF.Copy, [Sb], [nsbfb])
                yield

            yield
            abanks = [nb(hold="F_a0"), nb(hold="F_a1")]
            for hh in range(8):
                kvh = hh // 4
                abk, abb = abanks[kvh]
                for kti, (kname, nk, kcol) in enumerate(kts):
                    vslot = tc if kname == "prev" else tc + 1
                    if is_sample and kname == "prev":
                        vslot = 0
                    mm(abk[0:TC, (hh % 4) * 65:(hh % 4) * 65 + 65], PT[0:nk, kti, hh, 0:TC], Vaug[0:nk, vslot, kvh, :],
                       kti == 0, kti == len(kts) - 1, [A_bufs["PT"], Vaugb[vslot]], [abb], hh % 4 == 3 and kti == len(kts) - 1)
            yield
            oa, oab = oar.next()
            sq, sqb = statr.next()
            for kvh in range(2):
                abk, abb = abanks[kvh]
                av3 = abk[0:TC, 0:260].rearrange("p (h d) -> p h d", h=4)
                tt(sq[0:TC, kvh * 4:(kvh + 1) * 4].unsqueeze(2), av3[:, :, 64:65], esink[0:TC, kvh * 4:(kvh + 1) * 4].unsqueeze(2),
                   ALU.add, [abb, C], [sqb])
                dve(lambda e, sq=sq, kvh=kvh: e.reciprocal(out=sq[0:TC, 8 + kvh * 4:8 + (kvh + 1) * 4], in_=sq[0:TC, kvh * 4:(kvh + 1) * 4]),
                    [sqb], [sqb])
                tt(oa[0:TC, kvh * 256:(kvh + 1) * 256].rearrange("p (h d) -> p h d", h=4), av3[:, :, 0:64],
                   sq[0:TC, 8 + kvh * 4:8 + (kvh + 1) * 4].unsqueeze(2).to_broadcast([TC, 4, 64]), ALU.mult, [abb, sqb], [oab])
            nb_release("F_a0")
            nb_release("F_a1")
            yield
            bk, bb = nb()
            bkb = bk.bitcast(BF16)
            for j in range(4):
                tr(bkb[:, j * TC:(j + 1) * TC], oa[0:TC, j * 128:(j + 1) * 128], identb[0:TC, 0:TC], [oab, C], [bb], j == 3)
            act(mixT[:, 4:8, tsl(tc)], bkb[:, 0:4 * TC].rearrange("p (j t) -> p j t", j=4), AF.Copy, [bb], [mixTb[1]])

            yield
            sq, sqb = statr.next()
            for h in range(4):
                act(junk[0:TC, h * 128:(h + 1) * 128], obk[0:TC, h * 128:(h + 1) * 128], AF.Square, [obb], [junkb, sqb],
                    accum_out=sq[0:TC, h:h + 1])
            act(sq[0:TC, 4:8], sq[0:TC, 0:4], AF.Ln, [sqb, C], [sqb], scale=1.0 / 128.0, bias=epsb[0:TC, 1:2])
            act(sq[0:TC, 4:8], sq[0:TC, 4:8], AF.Exp, [sqb], [sqb], scale=-0.5)
            on, onb = onr.next()
            tt(on[0:TC], obk[0:TC, :].rearrange("p (h d) -> p h d", h=4), sq[0:TC, 4:8].unsqueeze(2).to_broadcast([TC, 4, 128]),
               ALU.mult, [obb, sqb], [onb])
            nb_release("F_o")
            yield
            bk, bb = nb()
            bkb = bk.bitcast(BF16)
            for h in range(4):
                tr(bkb[:, h * TC:(h + 1) * TC], on[0:TC, h, :], identb[0:TC, 0:TC], [onb, C], [bb], h == 3)
            stt(mixT[:, 0:4, tsl(tc)], bkb[:, 0:4 * TC].rearrange("p (h t) -> p h t", h=4), gnorm, shg[:, :, tsl(tc)],
                ALU.mult, ALU.mult, [bb, C, A_bufs["shg"]], [mixTb[0]])
            yield

        if not last and not is_sample:
            act(KT[:, :, 0:128], KT[:, :, TT:TT + 128], AF.Copy, [KTb], [KTb])
            act(Vaug[:, 0, :, 0:64], Vaug[:, NTC, :, 0:64], AF.Copy, [Vaugb[NTC]], [Vaugb[0]])

        yield "SPLIT"
        def res_x_reload(tc, half):
            xh, xhb = xhalf.next()
            S.dma("sp", xh[0:TC, :], x_rows(tc)[:, half * 512:(half + 1) * 512], writes=[xhb])
            return xh[0:TC, :], xhb
        yield from out_proj_residual(tcs, TC, lambda kc, tc: mixT[:, kc, tsl(tc)], mixTb, WO0, res_x_reload)
        x1T, x1Tb = actT_b
        ln1_def = []
        yield from layer_norm(tcs, TC, 0, x1T, x1Tb, ln1_def)

        for n_ in ("qmT", "PTm", "omT"):
            S.inherit(CD_bufs[n_], [CD_bufs["hT"]] + stg3b)

        for j in range(2):
            wv, wb = wload(WQ0 + j, 8, 512)
            for i in range(4):
                bk, bb = nb()
                for kc in range(8):
                    mm(bk[:, 0:TT], wv[:, kc, i * 128:(i + 1) * 128], x1T[:, kc, 0:TT], kc == 0, kc == 7, [wb, x1Tb[kc]], [bb], kc == 7)
                act(qmT[:, j * 4 + i, 0:TT], bk[:, 0:TT], AF.Copy, [bb], [CD_bufs["qmT"]], scale=1.0 / 16.0)
                yield
        for h in range(4):
            for mt in range(2):
                bk, bb = nb()
                for dc in range(2):
                    mm(bk[:, 0:TT], mkT[:, h * 2 + dc, mt * 128:(mt + 1) * 128], qmT[:, h * 2 + dc, 0:TT], dc == 0, dc == 1,
                       [mkTb, CD_bufs["qmT"]], [bb], dc == 1)
                act(PTm[:, mt, h, 0:TT], bk[:, 0:TT], AF.Exp, [bb], [CD_bufs["PTm"]])
            if ln1_def:
                ln1_def.pop(0)()
            yield
        for tc in tcs:
            yield
            obanks = [nb(), nb()]
            dbk, dbb = nb()
            for h in range(4):
                obk, obb = obanks[h // 2]
                for mt in range(2):
                    mm(obk[0:TC, (h % 2) * 256:(h % 2 + 1) * 256], PTm[:, mt, h, tsl(tc)], mvb[:, mt, h * 256:(h + 1) * 256],
                       mt == 0, mt == 1, [CD_bufs["PTm"], mvbb], [obb], h % 2 == 1 and mt == 1)
                for mt in range(2):
                    mm(dbk[0:TC, h:h + 1], PTm[:, mt, h, tsl(tc)], onescol[:, 0:1], mt == 0, mt == 1,
                       [CD_bufs["PTm"], C], [dbb], h == 3 and mt == 1)
            sq, sqb = statr.next()
            dve(lambda e, sq=sq, dbk=dbk: e.reciprocal(out=sq[0:TC, 0:4], in_=dbk[0:TC, 0:4]), [dbb], [sqb])
            om, omb = omr.next()
            for h in range(4):
                obk, obb = obanks[h // 2]
                act(om[0:TC, h * 256:(h + 1) * 256], obk[0:TC, (h % 2) * 256:(h % 2 + 1) * 256], AF.Identity, [obb, sqb], [omb],
                    scale=sq[0:TC, h:h + 1])
            yield
            bk, bb = nb()
            bkb = bk.bitcast(BF16)
            for kc in range(8):
                tr(bkb[:, kc * TC:(kc + 1) * TC], om[0:TC, kc * 128:(kc + 1) * 128], identb[0:TC, 0:TC], [omb, C], [bb], kc == 7)
            act(omT[:, :, tsl(tc)], bkb[:, 0:8 * TC].rearrange("p (k t) -> p k t", k=8), AF.Copy, [bb], [CD_bufs["omT"]])
        for d_ in ln1_def:
            d_()
        ln1_def.clear()
        yield from out_proj_residual(tcs, TC, lambda kc, tc: omT[:, kc, tsl(tc)], [CD_bufs["omT"]], WMO0,
                                     lambda tc, half: (resid[0:TC, tc, half * 512:(half + 1) * 512], residb[tc]))
        yield "CDONE"
        x2T, x2Tb = actT_b
        ln2_def = []
        yield from layer_norm(tcs, TC, 1, x2T, x2Tb, ln2_def)

        S.inherit(CD_bufs["hT"], [CD_bufs["qmT"], CD_bufs["PTm"], CD_bufs["omT"]] + stg3b)
        for j in range(11):
            yield
            wv, wb = wload(WF0 + j, 8, 512)
            for jj in range(2):
                cidx = 2 * j + jj
                gbk, gbb = nb()
                for kc in range(8):
                    mm(gbk[:, 0:TT], wv[:, kc, jj * 128:(jj + 1) * 128], x2T[:, kc, 0:TT], kc == 0, kc == 7, [wb, x2Tb[kc]], [gbb], kc == 7)
                ubk, ubb = nb()
                for kc in range(8):
                    mm(ubk[:, 0:TT], wv[:, kc, 256 + jj * 128:256 + (jj + 1) * 128], x2T[:, kc, 0:TT], kc == 0, kc == 7, [wb, x2Tb[kc]], [ubb], kc == 7)
                sg_, sgb = sgr.next()
                act(sg_[:, 0:TT], gbk[:, 0:TT], AF.Silu, [gbb], [sgb])
                tt(hT[:, cidx, 0:TT], sg_[:, 0:TT], ubk[:, 0:TT], ALU.mult, [sgb, ubb], [CD_bufs["hT"]])
                if ln2_def and cidx % 4 == 3:
                    ln2_def.pop(0)()
                yield
        for d_ in ln2_def:
            d_()
        ln2_def.clear()
        for half in range(2):
            for p0 in range(0, len(tcs), 2):
                pair = tcs[p0:p0 + 2]
                obanks = {tc_: nb(hold=("B_ffo", tc_)) for tc_ in pair}
                for gi, (k0, k1) in enumerate(FFO_GROUPS):
                    yield
                    wv, wb = wload(WFO0 + half * 3 + gi, k1 - k0, 512)
                    for tc in pair:
                        yield
                        obk, obb = obanks[tc]
                        for kc in range(k0, k1):
                            mm(obk[0:TC, :], hT[:, kc, tsl(tc)], wv[:, kc - k0, :], kc == 0, kc == 21, [CD_bufs["hT"], wb], [obb],
                               kc == k1 - 1)
                for tc in pair:
                    obk, obb = obanks[tc]
                    r = resid[0:TC, tc, half * 512:(half + 1) * 512]
                    stt(r, r, ALPHA, obk[0:TC, :], ALU.mult, ALU.add, [residb[tc], obb], [residb[tc]])
                    half_stats(TC, tc, half)
                    nb_release(("B_ffo", tc))
        yield from layer_norm(tcs, TC, 2, None, None, None)
        for tc in tcs:
            out_toks.append(S.dma("act", y_rows(tc), stg3[0:TC, tc, :], reads=[stg3b[tc]]))

    def drive(entries):
        prev = None
        for g, pre, post in entries:
            ratio = 1.55
            if pre is not None:
                if prev is not None:
                    for tag in prev:
                        if tag == "CDONE":
                            break
                    else:
                        prev = None
                    ratio = 0.8
                pre()
            front_done = False
            if prev is not None:
                acc = 0.0
                for _ in prev:
                    acc += 1.0 / ratio
                    while acc >= 1.0 and not front_done:
                        acc -= 1.0
                        if next(g) == "SPLIT":
                            front_done = True
            while not front_done:
                if next(g) == "SPLIT":
                    front_done = True
            if post is not None:
                post()
            prev = g
        if prev is not None:
            for _ in prev:
                pass

    def main_program():
        chk(0)
        entries = []
        for b in range(NB):
            def pre(b=b):
                mem_prepare(memp[b * MEM:(b + 1) * MEM, :], None, mkp[b * MEM:(b + 1) * MEM, :], mvp[b * MEM:(b + 1) * MEM, :], True)
                dve(lambda e: e.memset(Sst[:], 0.0), [], [Sb])
                act(Sbf[gch[0] % 3][0], Sst[:], AF.Copy, [Sb], [Sbf[gch[0] % 3][1]])

            def post(b=b):
                out_toks.append(S.dma("act", sp_o[b].rearrange("h k v -> k h v"), Sst[:], reads=[Sb]))
            for t in range(NTILE):
                r0 = b * SEQ + t * TTP
                last = (t == NTILE - 1)
                g = emit_tile(TTP, 128, 64,
                              lambda tc, r0=r0: xp[r0 + tc * 128:r0 + (tc + 1) * 128, :],
                              lambda tc, r0=r0: yp[r0 + tc * 128:r0 + (tc + 1) * 128, :],
                              t == 0, last, False, (kp[b], vp[b]) if last else None)
                entries.append((g, pre if t == 0 else None, post if last else None))

        if sample:
            def pre_s():
                mem_prepare(cmk, cmv, None, None, False)
                S.dma("sp", Sst[:], sh.rearrange("h k v -> k h v"), writes=[Sb])
                act(Sbf[gch[0] % 3][0], Sst[:], AF.Copy, [Sb], [Sbf[gch[0] % 3][1]])
                xt, xtb = xpool.next()
                ck3 = ck.rearrange("t (k d) -> t k d", k=2)
                for kvh in range(2):
                    for dup in range(2):
                        S.dma("sp", xt[:, kvh * 128 + dup * 64:kvh * 128 + (dup + 1) * 64], ck3[:, kvh, :], writes=[xtb])
                S.dma("sp", xt[:, 256:384], cv, writes=[xtb])
                bk, bb = nb()
                for kvh in range(2):
                    tr(bk[:, kvh * 128:(kvh + 1) * 128], xt[:, kvh * 128:(kvh + 1) * 128], identf[:], [xtb, C], [bb], kvh == 1)
                act(KT[:, :, 0:128], bk[:, 0:256].rearrange("p (k t) -> p k t", k=2), AF.Copy, [bb], [KTb])
                act(Vaug[:, 0, :, 0:64], xt[:, 256:384].rearrange("p (k d) -> p k d", k=2), AF.Copy, [xtb], [Vaugb[0]])

            def post_s():
                out_toks.append(S.dma("act", ss_o.rearrange("h k v -> k h v"), Sst[:], reads=[Sb]))
            entries.append((emit_tile(16, 16, 16, lambda tc: xs[:, :], lambda tc: ys[:, :], False, True, True, (ks_o, vs_o)),
                            pre_s, post_s))
        drive(entries)

    try:
        main_program()
    except _Stop:
        pass
    S.wait_all("act", out_toks + [b_.w for b_ in wsb if b_.w is not None])
    S.emit(st)
    st.close()
    return nc


_CACHE = {}


def _get_nc(NB, SEQ):
    key = (NB, SEQ)
    if key not in _CACHE:
        _CACHE[key] = build(NB, SEQ)
    return _CACHE[key]


def run(inputs, n_cores=N_CORES):
    f32 = lambda a: np.ascontiguousarray(np.asarray(a, dtype=np.float32))
    x_prompt = f32(inputs["x_prompt"])
    B, SEQ, _ = x_prompt.shape
    NB = B // n_cores
    nc = _get_nc(NB, SEQ)
    consts = _consts()
    shared = {
        "w_in": f32(inputs["w_in"][0]), "lbl": f32(inputs["hgrn_lb_logits"]), "gn": f32(inputs["hgrn_norm_g"]),
        "sinks": f32(inputs["attn_sinks"]), "w_out": f32(inputs["w_out"][0]), "w_mq": f32(inputs["w_mem_q"][0]),
        "w_mkv": f32(inputs["w_mem_kv"][0]), "w_mo": f32(inputs["w_mem_o"][0]), "w_fi": f32(inputs["w_ffn_in"][0]),
        "w_fo": f32(inputs["w_ffn_out"][0]), "ln_g": f32(inputs["ln_g"][0]), "ln_b": f32(inputs["ln_b"][0]),
    }
    shared.update(consts)
    mem_prompt = f32(inputs["mem_prompt"])
    x_sample = f32(inputs["x_sample"])
    ck = f32(inputs["cache_swa_k"][0]); cv = f32(inputs["cache_swa_v"][0])
    shs = f32(inputs["state_hgrn"][0])
    cmk = f32(inputs["cache_mem_k"][0]); cmv = f32(inputs["cache_mem_v"][0])
    in_maps = []
    for c in range(n_cores):
        m = dict(shared)
        m["xp"] = x_prompt[c * NB:(c + 1) * NB].reshape(NB * SEQ, D)
        m["memp"] = mem_prompt[c * NB:(c + 1) * NB].reshape(NB * MEM, D)
        m["xs"] = x_sample[c].reshape(16, D)
        m["ck"] = ck[c].reshape(128, 128)
        m["cv"] = cv[c].reshape(128, 128)
        m["sh"] = shs[c]
        m["cmk"] = cmk[c].reshape(MEM, D)
        m["cmv"] = cmv[c].reshape(MEM, D)
        in_maps.append(m)
    res = run_bass_kernel_spmd(nc, in_maps, core_ids=list(range(n_cores)))
    R = res.results
    cat = lambda name, shp: np.concatenate([np.asarray(r[name], dtype=np.float32).reshape(shp) for r in R], axis=0)
    yp = cat("yp", (NB, SEQ, D))
    ys = cat("ys", (1, 16, D))
    kp = cat("kp", (NB, 128, 2, 64))[None]
    vp = cat("vp", (NB, 128, 2, 64))[None]
    sp_o = cat("sp_o", (NB, 4, 128, 128))[None]
    mkp = cat("mkp", (NB, MEM, 4, 256))[None]
    mvp = cat("mvp", (NB, MEM, 4, 256))[None]
    ks = cat("ks_o", (1, 16, 2, 64))[None]
    vs = cat("vs_o", (1, 16, 2, 64))[None]
    ss = cat("ss_o", (1, 4, 128, 128))[None]
    return (yp, ys, kp, vp, sp_o, mkp, mvp, ks, vs, ss)


def kernel(**inputs):
    return run(inputs)
```

```python
import math
from contextlib import ExitStack

import numpy as np
import ml_dtypes

import concourse.bass as bass
import concourse.mybir as mybir
from concourse.bass_utils import run_bass_kernel_spmd

F32 = mybir.dt.float32
BF16 = mybir.dt.bfloat16
AF = mybir.ActivationFunctionType
ALU = mybir.AluOpType

ALPHA = 2.0 ** 0.25
N_CORES = 8
D = 1024
MEM = 256

ENGS = ("pe", "act", "dve", "pool", "sp")
SAME_ENGINE_RAW = {"pe": False, "act": True, "dve": True, "pool": True, "sp": False}


class Buf:
    __slots__ = ("name", "w", "r")

    def __init__(self, name=""):
        self.name = name
        self.w = None
        self.r = {}


class Sched:
    def __init__(self, nc, n_dma_sems=40):
        self.nc = nc
        self.streams = {e: [] for e in ENGS}
        self.cnt = {e: 0 for e in ENGS}
        self.waited = {e: {} for e in ENGS}
        self.sems = {}
        self.n_dma = n_dma_sems
        self.dma_uses = [0] * n_dma_sems
        self.dma_next = 0
        self.n_sw = 0

    def _need(self, eng, tok, waits, same_ok):
        if tok is None:
            return
        key, val = tok
        if key == eng and not same_ok:
            return
        if self.waited[eng].get(key, 0) >= val:
            return
        if key in ENGS and key != eng and self.cnt[key] < val:
            raise RuntimeError(f"{eng} needs {key}>={val}, only {self.cnt[key]} incs emitted")
        if key == eng and self.cnt[key] < val:
            raise RuntimeError(f"{eng} self-wait on pending inc {val}")
        if waits.get(key, 0) < val:
            waits[key] = val

    @staticmethod
    def _flat(bufs):
        out = []
        for b in bufs:
            if isinstance(b, (list, tuple)):
                out.extend(Sched._flat(b))
            else:
                out.append(b)
        return out

    def _deps(self, eng, reads, writes):
        waits = {}
        se = SAME_ENGINE_RAW[eng]
        for b in reads:
            self._need(eng, b.w, waits, se)
        for b in writes:
            self._need(eng, b.w, waits, False)
            for k, v in b.r.items():
                self._need(eng, (k, v), waits, se)
        for k, v in waits.items():
            self.waited[eng][k] = v
        return list(waits.items())

    @staticmethod
    def _mark(tok, reads, writes):
        for b in writes:
            b.w = tok
            b.r = {}
        for b in reads:
            if b.r.get(tok[0], 0) < tok[1]:
                b.r[tok[0]] = tok[1]

    def op(self, eng, fn, reads=(), writes=(), inc=True):
        reads, writes = self._flat(reads), self._flat(writes)
        waits = self._deps(eng, reads, writes)
        if inc:
            self.cnt[eng] += 1
            tok = (eng, self.cnt[eng])
        else:
            tok = (eng, self.cnt[eng] + 1)
        self.streams[eng].append((waits, fn, (eng, 1) if inc else None))
        self._mark(tok, reads, writes)
        return tok

    def dma(self, q, out, in_, reads=(), writes=()):
        reads, writes = self._flat(reads), self._flat(writes)
        waits = self._deps(q, reads, writes)
        if q == "pool":
            key = ("sw", self.n_sw)
            self.n_sw += 1
            tok = (key, 16)
        else:
            idx = self.dma_next
            self.dma_next = (self.dma_next + 1) % self.n_dma
            key = ("dma", idx)
            prev = self.dma_uses[idx] * 16
            if prev and self.waited[q].get(key, 0) < prev:
                waits.append((key, prev))
                self.waited[q][key] = prev
            self.dma_uses[idx] += 1
            tok = (key, self.dma_uses[idx] * 16)

        def fn(e, out=out, in_=in_):
            return e.dma_start(out=out, in_=in_)
        self.streams[q].append((waits, fn, (key, 16)))
        self._mark(tok, reads, writes)
        return tok

    def inherit(self, dst, srcs):
        for s in srcs:
            if s.w is not None and dst.r.get(s.w[0], 0) < s.w[1]:
                dst.r[s.w[0]] = s.w[1]
            for k, v in s.r.items():
                if dst.r.get(k, 0) < v:
                    dst.r[k] = v

    def wait_all(self, eng, toks):
        waits = {}
        for t in toks:
            self._need(eng, t, waits, True)
        for k, v in waits.items():
            self.waited[eng][k] = v
        self.streams[eng].append((list(waits.items()), None, None))

    def emit(self, stack):
        nc = self.nc
        for e in ("pe", "act", "dve", "pool"):
            self.sems[e] = stack.enter_context(nc.semaphore("s_" + e))
        for i in range(self.n_dma):
            self.sems[("dma", i)] = stack.enter_context(nc.semaphore(f"s_dma{i}"))
        for i in range(self.n_sw):
            self.sems[("sw", i)] = stack.enter_context(nc.semaphore(f"s_sw{i}"))
        block = stack.enter_context(nc.Block())
        sems = self.sems

        def replay(stream):
            def run(eng):
                for waits, fn, inc in stream:
                    for k, v in waits:
                        eng.wait_ge(sems[k], v)
                    if fn is None:
                        continue
                    ins = fn(eng)
                    if inc is not None:
                        ins.then_inc(sems[inc[0]], inc[1])
            return run

        block.tensor(replay(self.streams["pe"]))
        block.scalar(replay(self.streams["act"]))
        block.vector(replay(self.streams["dve"]))
        block.gpsimd(replay(self.streams["pool"]))
        block.sync(replay(self.streams["sp"]))


class Ring:
    def __init__(self, items):
        self.items = items
        self.i = 0

    def next(self):
        it = self.items[self.i]
        self.i = (self.i + 1) % len(self.items)
        return it


WI_HF, WI_HQ, WI_HG, WI_AQ, WI_AKD, WI_HI, WI_AVK = range(7)
WO0 = 7
WQ0 = 9
WMO0 = 11
WKV0 = 13
WF0 = 17
WFO0 = 28
NBLK = 34
FFO_GROUPS = ((0, 8), (8, 16), (16, 22))


def _consts():
    slopes = 2.0 ** (-8.0 * np.arange(1, 9) / 8.0)
    k = np.arange(128)[:, None, None]
    q = np.arange(128)[None, None, :]
    sl = slopes[None, :, None]
    kc, qc = k // 64, q // 64
    ep_prev = np.exp(-sl * (128 + q - k)) * ~((kc == 0) & (qc == 1))
    ep_cur = np.exp(-sl * np.abs(q - k)) * ~((kc == 1) & (qc == 0))
    qs = np.arange(16)[None, None, :]
    es_prev = np.exp(-sl * (128 + qs - k))
    ks = np.arange(16)[:, None, None]
    es_cur = np.zeros((128, 8, 16))
    es_cur[:16] = np.exp(-sl * np.abs(qs - ks))
    s = np.arange(128)[:, None]
    t = np.arange(128)[None, :]
    mask_p = ((s // 64 == t // 64) & (s <= t)).astype(np.float32)
    mask_s = np.zeros((128, 128), np.float32)
    mask_s[:16, :16] = (s[:16] <= t[:, :16])
    smask_p = (np.arange(512) % 64 == 0).astype(np.float32)
    smask_s = (np.arange(512) == 0).astype(np.float32)
    bf = ml_dtypes.bfloat16
    return {
        "c_identf": np.eye(128, dtype=np.float32),
        "c_identb": np.eye(128, dtype=np.float32).astype(bf),
        "c_maskp": mask_p,
        "c_epp": ep_prev.reshape(128, 1024).astype(bf),
        "c_epc": ep_cur.reshape(128, 1024).astype(bf),
    }


class _Stop(Exception):
    pass


DEBUG_STOP = [None]


def build(NB, SEQ, sample=True):
    def chk(n):
        if DEBUG_STOP[0] == n:
            raise _Stop()
    nc = bass.Bass("TRN2", target_bir_lowering=False)
    TTP = 512
    NTILE = SEQ // TTP
    assert SEQ % TTP == 0

    def din(name, shape, dt=F32):
        return nc.dram_tensor(name, list(shape), dt, kind="ExternalInput").ap()

    def dout(name, shape):
        return nc.dram_tensor(name, list(shape), F32, kind="ExternalOutput").ap()

    xp = din("xp", [NB * SEQ, D])
    memp = din("memp", [NB * MEM, D])
    xs = din("xs", [16, D])
    ck = din("ck", [128, 128])
    cv = din("cv", [128, 128])
    sh = din("sh", [4, 128, 128])
    cmk = din("cmk", [MEM, D])
    cmv = din("cmv", [MEM, D])
    w_in = din("w_in", [D, 2816])
    lbl = din("lbl", [2, 512])
    gn = din("gn", [1, 128])
    sinks = din("sinks", [1, 8])
    w_out = din("w_out", [D, D])
    w_mq = din("w_mq", [D, D])
    w_mkv = din("w_mkv", [D, 2 * D])
    w_mo = din("w_mo", [D, D])
    w_fi = din("w_fi", [D, 5632])
    w_fo = din("w_fo", [2816, D])
    ln_g = din("ln_g", [3, D])
    ln_b = din("ln_b", [3, D])
    c_identf = din("c_identf", [128, 128])
    c_identb = din("c_identb", [128, 128], BF16)
    c_maskp = din("c_maskp", [128, 128])
    c_epp = din("c_epp", [128, 1024], BF16)
    c_epc = din("c_epc", [128, 1024], BF16)

    yp = dout("yp", [NB * SEQ, D])
    ys = dout("ys", [16, D])
    kp = dout("kp", [NB, 128, 128])
    vp = dout("vp", [NB, 128, 128])
    sp_o = dout("sp_o", [NB, 4, 128, 128])
    mkp = dout("mkp", [NB * MEM, D])
    mvp = dout("mvp", [NB * MEM, D])
    ks_o = dout("ks_o", [16, 128])
    vs_o = dout("vs_o", [16, 128])
    ss_o = dout("ss_o", [4, 128, 128])

    ws = nc.dram_tensor("ws", [NBLK, 128, 4096], BF16, kind="Internal").ap()

    st = ExitStack()
    S = Sched(nc)
    out_toks = []

    def sb(name, shape, dt):
        return st.enter_context(nc.sbuf_tensor(name, list(shape), dt))

    identf = sb("identf", [128, 128], F32)
    identb = sb("identb", [128, 128], BF16)
    maskp = sb("maskp", [128, 128], F32)
    epp = sb("epp", [128, 8, 128], BF16)
    epc = sb("epc", [128, 8, 128], BF16)
    fst = sb("fst", [128, 512], F32)
    grep_ = sb("grep", [128, 3, D], F32)
    brep = sb("brep", [128, 3, D], F32)
    small = sb("small", [128, 64], F32)
    onescol = sb("onescol", [128, 2], BF16)
    epsb = sb("epsb", [128, 2], F32)
    gbT = sb("gbT", [128, 48], F32)
    C = Buf("consts")
    fstb = Buf("fst")
    gch = [0]

    NW = 3
    wring_t = sb("wring", [128, NW, 4096], BF16)
    wring = Ring([(wring_t[:, i, :], Buf(f"w{i}")) for i in range(NW)])
    xpool_t = sb("xpool", [128, 2, D], F32)
    xhb_ = [Buf(f"xh{i}") for i in range(4)]
    xpool = Ring([(xpool_t[:, i, :], [xhb_[2 * i], xhb_[2 * i + 1]]) for i in range(2)])
    actT_t = sb("actT", [128, 2, 8, 512], BF16)
    actT_f = (actT_t[:, 0], Buf("aTf"))
    actT_b = (actT_t[:, 1], [Buf(f"aTb{k_}") for k_ in range(8)])
    xrl_t = actT_t[:, 1].rearrange("p k t -> p (k t)").bitcast(F32).rearrange("p (j c) -> p j c", j=4)
    xhalf = Ring([(xrl_t[:, j, :], [actT_b[1][2 * j], actT_b[1][2 * j + 1]]) for j in range(4)])

    ARENA_B = 45056 + 24576
    arena = sb("arena", [128, ARENA_B // 4], F32)

    def carve(off, nbytes, dt, pat=None, **kw):
        a = arena[:, off // 4:(off + nbytes) // 4]
        if dt == BF16:
            a = a.bitcast(BF16)
        if pat:
            a = a.rearrange(pat, **kw)
        return a
    KB = 1024
    eG = carve(0, 8 * KB, F32, "p (h t) -> p h t", h=4)
    qgT = carve(8 * KB, 4 * KB, BF16, "p (h t) -> p h t", h=4)
    kgT = carve(12 * KB, 4 * KB, BF16, "p (h t) -> p h t", h=4)
    kdecT = carve(16 * KB, 4 * KB, BF16, "p (h t) -> p h t", h=4)
    shg = carve(20 * KB, 4 * KB, BF16, "p (h t) -> p h t", h=4)
    QT = carve(24 * KB, 4 * KB, BF16, "p (h t) -> p h t", h=4)
    vtok = carve(28 * KB, 4 * KB, BF16, "p (c f) -> p c f", c=4)
    gtmp_t = carve(32 * KB, 8 * KB, F32, "p (n t) -> p n t", n=4)
    PT = carve(40 * KB, 4 * KB, BF16, "p (k h q) -> p k h q", k=2, h=8)
    CD0 = 44 * KB
    qmT = carve(CD0, 8 * KB, BF16, "p (c t) -> p c t", c=8)
    PTm = carve(CD0 + 8 * KB, 8 * KB, BF16, "p (m h t) -> p m h t", m=2, h=4)
    omT = carve(CD0 + 16 * KB, 8 * KB, BF16, "p (c t) -> p c t", c=8)
    hT = carve(CD0, 22 * KB, BF16, "p (c t) -> p c t", c=22)
    stg3 = carve(CD0, 16 * KB, F32, "p (c f) -> p c f", c=4)
    stg3b = [Buf(f"stg3_{i}") for i in range(4)]
    A_bufs = {n: Buf(n) for n in ["eG", "qgT", "kgT", "kdecT", "shg", "QT", "vtok", "PT", "g0", "g1", "g2", "g3"]}
    CD_bufs = {n: Buf(n) for n in ["qmT", "PTm", "omT", "hT"]}
    gtmp = Ring([(gtmp_t[:, i, :], A_bufs[f"g{i}"]) for i in range(4)])

    KT = sb("KT", [128, 2, 128 + 512], BF16)
    KTb = Buf("KT")
    Vaug = sb("Vaug", [128, 5, 2, 65], BF16)
    Vaugb = [Buf(f"va{i}") for i in range(5)]
    AT_t = sb("AT", [128, 2, 4, 128], BF16)
    ATr = Ring([(AT_t[:, i], Buf(f"AT{i}")) for i in range(2)])
    kdt_t = sb("kdt", [128, 2, 4, 128], BF16)
    kdtr = Ring([(kdt_t[:, i], Buf(f"kdt{i}")) for i in range(2)])
    Sst = sb("Sst", [128, 4, 128], F32)
    Sb = [Buf(f"S{h_}") for h_ in range(4)]
    Sbf_t = sb("Sbf", [128, 3, 4, 128], BF16)
    Sbf = [(Sbf_t[:, i], Buf(f"Sbf{i}")) for i in range(3)]
    on_t = sb("on", [128, 2, 4, 128], BF16)
    onr = Ring([(on_t[:, i], Buf(f"on{i}")) for i in range(2)])
    pt_t = sb("pt", [128, 2, 4, 128], BF16)
    ptr_ = Ring([(pt_t[:, i], Buf(f"pt{i}")) for i in range(2)])
    oa_t = sb("oa", [128, 2, 512], BF16)
    oar = Ring([(oa_t[:, i], Buf(f"oa{i}")) for i in range(2)])
    junk = sb("junk", [128, 512], F32)
    junkb = Buf("junk")
    lnst_t = sb("lnst", [128, 2, 4, 16], F32)
    lnring = Ring([(lnst_t[:, i], Buf(f"lnst{i}")) for i in range(2)])
    lncur = [lnring.next()]

    def half_stats(TC, tc, half):
        lnst, lnstb = lncur[0]
        dve(lambda e: e.bn_stats(out=lnst[0:TC, tc, half * 6:(half + 1) * 6], in_=resid[0:TC, tc, half * 512:(half + 1) * 512]),
            [residb[tc]], [lnstb])
    stat = sb("stat", [128, 8, 16], F32)
    statr = Ring([(stat[:, i, :], Buf(f"stat{i}")) for i in range(8)])
    mixT = sb("mixT", [128, 8, 512], BF16)
    mixTb = [Buf("mixT_h"), Buf("mixT_a")]
    resid = sb("resid", [128, 4, D], F32)
    residb = [Buf(f"res{i}") for i in range(4)]
    mkT = sb("mkT", [128, 8, 256], BF16)
    mkTb = Buf("mkT")
    mvb = sb("mvb", [128, 2, D], BF16)
    mvbb = Buf("mvb")
    memT = carve(32 * KB, 4 * KB, BF16, "p (c m) -> p c m", c=8)
    memTbs = [A_bufs["g0"], A_bufs["g1"]]
    stgr = Ring([(gtmp_t[:, 2, :], A_bufs["g2"]), (gtmp_t[:, 3, :], A_bufs["g3"])])
    om_t = sb("om", [128, 2, D], BF16)
    omr = Ring([(om_t[:, i, :], Buf(f"om{i}")) for i in range(2)])
    sg_t = sb("sg", [128, 2, 512], BF16)
    sgr = Ring([(sg_t[:, i, :], Buf(f"sg{i}")) for i in range(2)])

    banks_t = [st.enter_context(nc.psum_tensor(f"bank{i}", [128, 512], F32)) for i in range(8)]
    bank_list = [(banks_t[i], Buf(f"bank{i}")) for i in range(8)]
    bank_held = {}
    bank_ptr = [0]

    def nb(hold=None):
        for _ in range(9):
            i = bank_ptr[0]
            bank_ptr[0] = (i + 1) % 8
            if i not in bank_held.values():
                if hold is not None:
                    bank_held[hold] = i
                t, b = bank_list[i]
                return t[:], b
        raise RuntimeError("no free PSUM bank")

    def nb_release(key):
        bank_held.pop(key, None)

    def mm(out, lhsT, rhs, start, stop, reads, writes, inc):
        S.op("pe", lambda e: e.matmul(out, lhsT=lhsT, rhs=rhs, start=start, stop=stop), reads, writes, inc)

    def tr(out, in_, ident, reads, writes, inc):
        S.op("pe", lambda e: e.transpose(out, in_, ident), reads, writes, inc)

    def act(out, in_, func, reads, writes, **kw):
        S.op("act", lambda e: e.activation(out=out, in_=in_, func=func, **kw), reads, writes)

    def tt(out, in0, in1, op, reads, writes):
        S.op("dve", lambda e: e.tensor_tensor(out=out, in0=in0, in1=in1, op=op), reads, writes)

    def ptt(out, in0, in1, op, reads, writes):
        S.op("pool", lambda e: e.tensor_tensor(out=out, in0=in0, in1=in1, op=op), reads, writes)

    def ts(out, in0, s1, s2, op0, op1, reads, writes):
        S.op("dve", lambda e: e.tensor_scalar(out=out, in0=in0, scalar1=s1, scalar2=s2, op0=op0, op1=op1), reads, writes)

    def stt(out, in0, scalar, in1, op0, op1, reads, writes):
        S.op("dve", lambda e: e.scalar_tensor_tensor(out=out, in0=in0, scalar=scalar, in1=in1, op0=op0, op1=op1), reads, writes)

    def dve(fn, reads, writes):
        S.op("dve", fn, reads, writes)

    for dst, src in ((identf, c_identf), (identb, c_identb), (maskp, c_maskp),
                     (epp, c_epp.rearrange("p (h q) -> p h q", h=8)), (epc, c_epc.rearrange("p (h q) -> p h q", h=8))):
        S.dma("sp", dst[:], src, writes=[C])
    for i in range(3):
        S.dma("sp", grep_[:, i, :], ln_g[i:i + 1, :].partition_broadcast(128), writes=[C])
        S.dma("sp", brep[:, i, :], ln_b[i:i + 1, :].partition_broadcast(128), writes=[C])
    for li_ in range(3):
        for kc_ in range(8):
            S.dma("sp", gbT[:, li_ * 8 + kc_:li_ * 8 + kc_ + 1], ln_g[li_:li_ + 1, kc_ * 128:(kc_ + 1) * 128].rearrange("o p -> p o"), writes=[C])
            S.dma("sp", gbT[:, 24 + li_ * 8 + kc_:24 + li_ * 8 + kc_ + 1], ln_b[li_:li_ + 1, kc_ * 128:(kc_ + 1) * 128].rearrange("o p -> p o"), writes=[C])
    for l_ in range(2):
        for h_ in range(4):
            S.dma("sp", small[:, l_ * 4 + h_:l_ * 4 + h_ + 1], lbl[l_:l_ + 1, h_ * 128:(h_ + 1) * 128].rearrange("o p -> p o"), writes=[C])
    S.dma("sp", small[:, 16:17], gn.rearrange("o p -> p o"), writes=[C])
    S.dma("sp", small[:, 24:32], sinks.partition_broadcast(128), writes=[C])
    dve(lambda e: e.memset(fst[:], 0.0), [], [fstb])
    dve(lambda e: e.memset(onescol[:], 1.0), [], [C])
    dve(lambda e: e.memset(Vaug[:], 1.0), [], Vaugb)
    dve(lambda e: e.memset(epsb[:, 0:1], 1e-5), [], [C])
    dve(lambda e: e.memset(epsb[:, 1:2], 1e-6), [], [C])
    tt(small[:, 32:36], small[:, 4:8], small[:, 0:4], ALU.subtract, [C], [C])
    act(small[:, 32:36], small[:, 32:36], AF.Exp, [C], [C])
    ts(small[:, 32:36], small[:, 32:36], 1.0, None, ALU.add, ALU.bypass, [C], [C])
    dve(lambda e: e.reciprocal(out=small[:, 36:40], in_=small[:, 32:36]), [C], [C])
    ts(small[:, 8:12], small[:, 36:40], 0.5, 0.5, ALU.mult, ALU.add, [C], [C])
    ts(small[:, 12:16], small[:, 36:40], -0.5, 0.5, ALU.mult, ALU.add, [C], [C])
    act(small[:, 24:32], small[:, 24:32], AF.Exp, [C], [C])
    ca, cb, gnorm, esink = small[:, 8:12], small[:, 12:16], small[:, 16:17], small[:, 24:32]

    wsb = [Buf(f"ws{i}") for i in range(NBLK)]

    def cast(blk, k0, k1, c0, c1, src, s0, s1, ncols):
        nk = k1 - k0
        dst = ws[blk, :, 0:nk * ncols].rearrange("p (k c) -> p k c", c=ncols)[:, :, c0:c1]
        S.dma("pool", dst, src.rearrange("(k p) c -> p k c", p=128)[:, k0:k1, s0:s1], writes=[wsb[blk]])

    cast(WI_HF, 0, 8, 0, 512, w_in, 512, 1024, 512)
    cast(WI_HQ, 0, 8, 0, 512, w_in, 0, 512, 512)
    cast(WI_HG, 0, 8, 0, 512, w_in, 1536, 2048, 512)
    cast(WI_AQ, 0, 8, 0, 512, w_in, 2048, 2560, 512)
    for j in range(4):
        kvh = j // 2
        cast(WI_AKD, 0, 8, j * 64, (j + 1) * 64, w_in, 2560 + kvh * 64, 2560 + (kvh + 1) * 64, 256)
    cast(WI_HI, 0, 8, 0, 512, w_in, 1024, 1536, 512)
    cast(WI_AVK, 0, 8, 0, 128, w_in, 2688, 2816, 256)
    cast(WI_AVK, 0, 8, 128, 256, w_in, 2560, 2688, 256)
    for h in range(2):
        cast(WO0 + h, 0, 8, 0, 512, w_out, h * 512, (h + 1) * 512, 512)
        cast(WQ0 + h, 0, 8, 0, 512, w_mq, h * 512, (h + 1) * 512, 512)
        cast(WMO0 + h, 0, 8, 0, 512, w_mo, h * 512, (h + 1) * 512, 512)
    for j in range(4):
        cast(WKV0 + j, 0, 8, 0, 512, w_mkv, j * 512, (j + 1) * 512, 512)
    for j in range(11):
        cast(WF0 + j, 0, 8, 0, 256, w_fi, 256 * j, 256 * j + 256, 512)
        cast(WF0 + j, 0, 8, 256, 512, w_fi, 2816 + 256 * j, 2816 + 256 * j + 256, 512)
    for h in range(2):
        for gi, (k0, k1) in enumerate(FFO_GROUPS):
            cast(WFO0 + h * 3 + gi, k0, k1, 0, 512, w_fo, h * 512, (h + 1) * 512, 512)

    w_held = {}
    w_ptr = [0]

    def wload(blk, nk, ncols, who="B"):
        w_held.pop(who, None)
        for _ in range(NW + 1):
            i = w_ptr[0]
            w_ptr[0] = (i + 1) % NW
            if i not in w_held.values():
                break
        else:
            raise RuntimeError("no free weight slot")
        w_held[who] = i
        wt, wb = wring.items[i]
        n = nk * ncols
        S.dma("sp", wt[:, 0:n], ws[blk, :, 0:n], reads=[wsb[blk]], writes=[wb])
        return wt[:, 0:n].rearrange("p (k c) -> p k c", c=ncols), wb

    def transpose_rows_f32(src, srcb, TC, dstT, dstb, tcol0):
        for half in range(2):
            bk, bb = nb()
            for j in range(4):
                kc = half * 4 + j
                tr(bk[:, j * TC:(j + 1) * TC], src[0:TC, kc * 128:(kc + 1) * 128], identf[0:TC, 0:TC],
                   [srcb, C], [bb], j == 3)
            act(dstT[:, half * 4:half * 4 + 4, tcol0:tcol0 + TC],
                bk[:, 0:4 * TC].rearrange("p (j t) -> p j t", t=TC), AF.Copy, [bb], dstb if isinstance(dstb, list) else [dstb])

    def layer_norm(tcs, TC, li, aTout, aTb, deferred):
        n_ = len(tcs)
        lnst, lnstb = lncur[0]
        lncur[0] = lnring.next()
        for tc in tcs:
            dve(lambda e, tc=tc: e.bn_aggr(out=lnst[0:TC, tc, 12:14], in_=lnst[0:TC, tc, 0:12]), [lnstb], [lnstb])
        yield
        act(lnst[0:TC, 0:n_, 14:15], lnst[0:TC, 0:n_, 13:14], AF.Ln, [lnstb, C], [lnstb], bias=epsb[0:TC, 0:1])
        act(lnst[0:TC, 0:n_, 14:15], lnst[0:TC, 0:n_, 14:15], AF.Exp, [lnstb], [lnstb], scale=-0.5)
        stt(lnst[0:TC, 0:n_, 15:16], lnst[0:TC, 0:n_, 12:13], -1.0, lnst[0:TC, 0:n_, 14:15], ALU.mult, ALU.mult, [lnstb], [lnstb])
        for tc in tcs:
            r = resid[0:TC, tc, :]
            rb = residb[tc]
            act(r, r, AF.Identity, [rb, lnstb], [rb], scale=lnst[0:TC, tc, 14:15], bias=lnst[0:TC, tc, 15:16])
            yield
        if aTout is not None:
            for kc in range(8):
                yield
                bk, bb = nb()
                for tc in tcs:
                    tr(bk[:, tc * TC:(tc + 1) * TC], resid[0:TC, tc, kc * 128:(kc + 1) * 128], identf[0:TC, 0:TC],
                       [residb[tc], C], [bb], tc == tcs[-1])
                gcol = gbT[:, li * 8 + kc:li * 8 + kc + 1]
                bcol = gbT[:, 24 + li * 8 + kc:24 + li * 8 + kc + 1]
                ncol = len(tcs) * TC
                if kc % 2 == 0:
                    act(aTout[:, kc, 0:ncol], bk[:, 0:ncol], AF.Identity, [bb, C], [aTb[kc]], scale=gcol, bias=bcol)
                else:
                    ts(aTout[:, kc, 0:ncol], bk[:, 0:ncol], gcol, bcol, ALU.mult, ALU.add, [bb, C], [aTb[kc]])
        for tc in tcs:
            def affine(tc=tc):
                r = resid[0:TC, tc, :]
                rb = residb[tc]
                ptt(r, r, grep_[0:TC, li, :], ALU.mult, [rb, C], [rb])
                ptt(r, r, brep[0:TC, li, :], ALU.add, [rb, C], [rb])
            if deferred is None:
                S.inherit(stg3b[tc], [CD_bufs["hT"]])
                tt(stg3[0:TC, tc, :], resid[0:TC, tc, :], grep_[0:TC, li, :], ALU.mult, [residb[tc], C], [stg3b[tc]])
                ptt(stg3[0:TC, tc, :], stg3[0:TC, tc, :], brep[0:TC, li, :], ALU.add, [stg3b[tc], C], [stg3b[tc]])
                yield
            else:
                deferred.append(affine)

    def out_proj_residual(tcs, TC, lhsT_of, lhs_bufs, wblk0, res_in):
        for half in range(2):
            wv, wb = wload(wblk0 + half, 8, 512)
            for tc in tcs:
                bk, bb = nb()
                for kc in range(8):
                    mm(bk[0:TC, :], lhsT_of(kc, tc), wv[:, kc, :], kc == 0, kc == 7, lhs_bufs + [wb], [bb], kc == 7)
                rin, rinb = res_in(tc, half)
                stt(resid[0:TC, tc, half * 512:(half + 1) * 512], rin, ALPHA, bk[0:TC, :],
                    ALU.mult, ALU.add, [rinb, bb], [residb[tc]])
                half_stats(TC, tc, half)
                yield

    def mem_prepare(src_k, src_v, kout, vout, compute):
        if compute:
            for mt in range(2):
                xt, xtb = xpool.next()
                S.dma("sp", xt[:, :], src_k[mt * 128:(mt + 1) * 128, :], writes=[xtb])
                transpose_rows_f32(xt, xtb, 128, memT, memTbs, mt * 128)
            for j in range(4):
                wv, wb = wload(WKV0 + j, 8, 512)
                for mt in range(2):
                    bk, bb = nb()
                    for kc in range(8):
                        mm(bk, memT[:, kc, mt * 128:(mt + 1) * 128], wv[:, kc, :], kc == 0, kc == 7, memTbs + [wb], [bb], kc == 7)
                    sg_, sgb = stgr.next()
                    act(sg_, bk, AF.Copy, [bb], [sgb])
                    dst = kout if j < 2 else vout
                    out_toks.append(S.dma("act", dst[mt * 128:(mt + 1) * 128, (j % 2) * 512:(j % 2 + 1) * 512], sg_, reads=[sgb]))
                    if j < 2:
                        b2, b2b = nb()
                        for i in range(4):
                            tr(b2[:, i * 128:(i + 1) * 128], sg_[:, i * 128:(i + 1) * 128], identf[:], [sgb, C], [b2b], i == 3)
                        act(mkT[:, j * 4:(j + 1) * 4, mt * 128:(mt + 1) * 128], b2.rearrange("p (i m) -> p i m", i=4), AF.Copy, [b2b], [mkTb])
                    else:
                        dve(lambda e, mt=mt, j=j, sg_=sg_: e.tensor_copy(out=mvb[:, mt, (j - 2) * 512:(j - 1) * 512], in_=sg_), [sgb], [mvbb])
        else:
            for mt in range(2):
                xt, xtb = xpool.next()
                S.dma("sp", xt[:, :], src_k[mt * 128:(mt + 1) * 128, :], writes=[xtb])
                transpose_rows_f32(xt, xtb, 128, mkT, mkTb, mt * 128)
                xt2, xt2b = xpool.next()
                S.dma("sp", xt2[:, :], src_v[mt * 128:(mt + 1) * 128, :], writes=[xt2b])
                act(mvb[:, mt, :], xt2[:, :], AF.Copy, [xt2b], [mvbb])

    def emit_tile(TT, TC, L, x_rows, y_rows, first, last, is_sample, kv_out):
        NTC = TT // TC
        CP = TC // L
        NCH = TT // L
        tcs = list(range(NTC))
        maskA = maskp
        Eprev = epp
        Ecur = epc
        NKP = 128

        def tsl(tc):
            return slice(tc * TC, (tc + 1) * TC)


        xT, xTb = actT_f
        for tc in tcs:
            xt, xtb = xpool.next()
            S.dma("sp", xt[0:TC, :], x_rows(tc), writes=[xtb])
            transpose_rows_f32(xt, xtb, TC, xT, xTb, tc * TC)
            yield

        def fm_proj(blk, nchunks, ncols, consume):
            wv, wb = wload(blk, 8, ncols, "F")
            for j in range(nchunks):
                bk, bb = nb()
                for kc in range(8):
                    mm(bk[:, 0:TT], wv[:, kc, j * 128:(j + 1) * 128], xT[:, kc, 0:TT], kc == 0, kc == 7, [wb, xTb], [bb], kc == 7)
                r_ = consume(j, bk[:, 0:TT], bb)
                if r_ is not None:
                    yield from r_
                yield

        def c_hf(h, z, zb):
            t1, t1b = gtmp.next()
            t1 = t1[:, 0:TT]
            act(t1, z, AF.Tanh, [zb], [t1b], scale=0.5)
            ts(t1, t1, cb[:, h:h + 1], ca[:, h:h + 1], ALU.mult, ALU.add, [t1b, C], [t1b])
            f3 = t1.rearrange("p (c l) -> p c l", l=L)
            dve(lambda e: e.tensor_copy(out=fst[:, 0:TT].rearrange("p (c l) -> p c l", l=L)[:, :, 0:1], in_=f3[:, :, 0:1]), [t1b], [fstb])
            eg = eG[:, h, 0:TT]
            dve(lambda e: e.tensor_tensor_scan(out=eg, data0=t1, data1=fst[:, 0:TT], initial=1.0, op0=ALU.mult, op1=ALU.max),
                [t1b, fstb], [A_bufs["eG"]])
            t2, t2b = gtmp.next()
            t2 = t2[:, 0:TT]
            yield
            dve(lambda e: e.reciprocal(out=t2, in_=eg), [A_bufs["eG"]], [t2b])
            yield
            t3, t3b = gtmp.next()
            t3 = t3[:, 0:TT]
            stt(t3, t1, -1.0, t2, ALU.add, ALU.mult, [t1b, t2b], [t3b])
            act(kgT[:, h, 0:TT], t3, AF.Copy, [t3b], [A_bufs["kgT"]], scale=-1.0)
            egl = eg.rearrange("p (c l) -> p c l", l=L)[:, :, L - 1:L].to_broadcast([128, NCH, L])
            stt(kdecT[:, h, 0:TT].rearrange("p (c l) -> p c l", l=L), t3.rearrange("p (c l) -> p c l", l=L), -1.0, egl,
                ALU.mult, ALU.mult, [t3b, A_bufs["eG"]], [A_bufs["kdecT"]])

        def c_hq(h, z, zb):
            t1, t1b = gtmp.next()
            t1 = t1[:, 0:TT]
            act(t1, z, AF.Silu, [zb], [t1b])
            ptt(qgT[:, h, 0:TT], t1, eG[:, h, 0:TT], ALU.mult, [t1b, A_bufs["eG"]], [A_bufs["qgT"]])

        def c_hg(h, z, zb):
            act(shg[:, h, 0:TT], z, AF.Silu, [zb], [A_bufs["shg"]])

        def c_aq(j, z, zb):
            act(QT[:, j, 0:TT], z, AF.Copy, [zb], [A_bufs["QT"]], scale=0.125)

        def c_akd(j, z, zb):
            act(KT[:, j, NKP:NKP + TT], z, AF.Copy, [zb], [KTb])

        yield from fm_proj(WI_HG, 4, 512, c_hg)
        yield from fm_proj(WI_AQ, 4, 512, c_aq)
        yield from fm_proj(WI_AKD, 2, 256, c_akd)
        yield from fm_proj(WI_HF, 4, 512, c_hf)
        yield from fm_proj(WI_HQ, 4, 512, c_hq)

        wv, wb = wload(WI_HI, 8, 512, "F")
        for tc in tcs:
            bk, bb = nb()
            for kc in range(8):
                mm(bk[0:TC, :], xT[:, kc, tsl(tc)], wv[:, kc, :], kc == 0, kc == 7, [xTb, wb], [bb], kc == 7)
            act(vtok[0:TC, tc, :], bk[0:TC, :], AF.Copy, [bb], [A_bufs["vtok"]])
            yield
        wv, wb = wload(WI_AVK, 8, 256, "F")
        for tc in tcs:
            bk, bb = nb()
            for kc in range(8):
                mm(bk[0:TC, 0:256], xT[:, kc, tsl(tc)], wv[:, kc, :], kc == 0, kc == 7, [xTb, wb], [bb], kc == 7)
            act(Vaug[0:TC, tc + 1, :, 0:64], bk[0:TC, 0:128].rearrange("p (k d) -> p k d", k=2), AF.Copy, [bb], [Vaugb[tc + 1]])
            if kv_out is not None and tc == NTC - 1:
                kvo_t, kvob = gtmp.next()
                act(kvo_t[0:TC, 0:256], bk[0:TC, 0:256], AF.Copy, [bb], [kvob])
                out_toks.append(S.dma("act", kv_out[1], kvo_t[0:TC, 0:128], reads=[kvob]))
                out_toks.append(S.dma("act", kv_out[0], kvo_t[0:TC, 128:256], reads=[kvob]))
            yield

        for tc in tcs:
            bk, bb = nb()
            for h in range(4):
                mm(bk[0:TC, h * TC:(h + 1) * TC], kgT[:, h, tsl(tc)], qgT[:, h, tsl(tc)], True, True,
                   [A_bufs["kgT"], A_bufs["qgT"]], [bb], h == 3)
            AT, ATb = ATr.next()
            tt(AT[0:TC, :, 0:TC], bk[0:TC, 0:4 * TC].rearrange("p (h t) -> p h t", h=4),
               maskA[0:TC, 0:TC].unsqueeze(1).to_broadcast([TC, 4, TC]), ALU.mult, [bb, C], [ATb])
            bk, bb = nb()
            bkb = bk.bitcast(BF16)
            for h in range(4):
                tr(bkb[0:TC, h * 128:(h + 1) * 128], kdecT[:, h, tsl(tc)], identb[:], [A_bufs["kdecT"], C], [bb], h == 3)
            kdt, kdtb = kdtr.next()
            act(kdt[0:TC], bkb[0:TC, 0:512].rearrange("p (h d) -> p h d", h=4), AF.Copy, [bb], [kdtb])

            use_prev = (not first) or tc > 0
            kts = ([("prev", NKP, tc * TC if not is_sample else 0)] if use_prev else []) + [("cur", TC, NKP + tc * TC)]
            for kti, (kname, nk, kcol) in enumerate(kts):
                E = Eprev if kname == "prev" else Ecur
                for rt in range(2):
                    bk, bb = nb()
                    for kvh in range(2):
                        for gg in range(2):
                            idx = kvh * 2 + gg
                            mm(bk[0:nk, idx * TC:(idx + 1) * TC], KT[rt * 64:(rt + 1) * 64, kvh, kcol:kcol + nk],
                               QT[rt * 64:(rt + 1) * 64, kvh * 2 + gg, tsl(tc)], True, True, [KTb, A_bufs["QT"]], [bb], idx == 3)
                    pt, ptb = ptr_.next()
                    act(pt[0:nk, :, 0:TC], bk[0:nk, 0:4 * TC].rearrange("p (g q) -> p g q", g=4), AF.Exp, [bb], [ptb])
                    tt(PT[0:nk, kti, rt:8:2, 0:TC], pt[0:nk, :, 0:TC], E[0:nk, rt:8:2, 0:TC], ALU.mult,
                       [ptb, C], [A_bufs["PT"]])
            yield
            obk, obb = nb(hold="F_o")
            for ci in range(CP):
                c = tc * CP + ci
                rows = slice(ci * L, (ci + 1) * L)
                toks = slice(tc * TC + ci * L, tc * TC + (ci + 1) * L)
                sbf, sbfb = Sbf[gch[0] % 3]
                for h in range(4):
                    o_out = obk[rows, h * 128:(h + 1) * 128]
                    mm(o_out, qgT[:, h, toks], sbf[:, h, :], True, False, [A_bufs["qgT"], sbfb], [obb], False)
                    mm(o_out, AT[rows, h, ci * L:(ci + 1) * L], vtok[rows, tc, h * 128:(h + 1) * 128], False, True,
                       [ATb, A_bufs["vtok"]], [obb], h == 3 and ci == CP - 1)
                ubk, ubb = nb()
                for h in range(4):
                    mm(ubk[:, h * 128:(h + 1) * 128], kdt[rows, h, :], vtok[rows, tc, h * 128:(h + 1) * 128], True, True,
                       [kdtb, A_bufs["vtok"]], [ubb], h == 3)
                for h in range(4):
                    col = c * L + L - 1
                    stt(Sst[:, h, :], Sst[:, h, :], eG[:, h, col:col + 1], ubk[:, h * 128:(h + 1) * 128], ALU.mult, ALU.add,
                        [Sb[h], A_bufs["eG"], ubb], [Sb[h]])
                gch[0] += 1
                nsbf, nsbfb = Sbf[gch[0] % 3]
                act(nsbf, Sst[:], AF.Copy, [Sb], [nsbfb])
                yield

            yield
            abanks = [nb(hold="F_a0"), nb(hold="F_a1")]
            for hh in range(8):
                kvh = hh // 4
                abk, abb = abanks[kvh]
                for kti, (kname, nk, kcol) in enumerate(kts):
                    vslot = tc if kname == "prev" else tc + 1
                    if is_sample and kname == "prev":
                        vslot = 0
                    mm(abk[0:TC, (hh % 4) * 65:(hh % 4) * 65 + 65], PT[0:nk, kti, hh, 0:TC], Vaug[0:nk, vslot, kvh, :],
                       kti == 0, kti == len(kts) - 1, [A_bufs["PT"], Vaugb[vslot]], [abb], hh % 4 == 3 and kti == len(kts) - 1)
            yield
            oa, oab = oar.next()
            sq, sqb = statr.next()
            for kvh in range(2):
                abk, abb = abanks[kvh]
                av3 = abk[0:TC, 0:260].rearrange("p (h d) -> p h d", h=4)
                tt(sq[0:TC, kvh * 4:(kvh + 1) * 4].unsqueeze(2), av3[:, :, 64:65], esink[0:TC, kvh * 4:(kvh + 1) * 4].unsqueeze(2),
                   ALU.add, [abb, C], [sqb])
                dve(lambda e, sq=sq, kvh=kvh: e.reciprocal(out=sq[0:TC, 8 + kvh * 4:8 + (kvh + 1) * 4], in_=sq[0:TC, kvh * 4:(kvh + 1) * 4]),
                    [sqb], [sqb])
                tt(oa[0:TC, kvh * 256:(kvh + 1) * 256].rearrange("p (h d) -> p h d", h=4), av3[:, :, 0:64],
                   sq[0:TC, 8 + kvh * 4:8 + (kvh + 1) * 4].unsqueeze(2).to_broadcast([TC, 4, 64]), ALU.mult, [abb, sqb], [oab])
            nb_release("F_a0")
            nb_release("F_a1")
            yield
            bk, bb = nb()
            bkb = bk.bitcast(BF16)
            for j in range(4):
                tr(bkb[:, j * TC:(j + 1) * TC], oa[0:TC, j * 128:(j + 1) * 128], identb[0:TC, 0:TC], [oab, C], [bb], j == 3)
            act(mixT[:, 4:8, tsl(tc)], bkb[:, 0:4 * TC].rearrange("p (j t) -> p j t", j=4), AF.Copy, [bb], [mixTb[1]])

            yield
            sq, sqb = statr.next()
            for h in range(4):
                act(junk[0:TC, h * 128:(h + 1) * 128], obk[0:TC, h * 128:(h + 1) * 128], AF.Square, [obb], [junkb, sqb],
                    accum_out=sq[0:TC, h:h + 1])
            act(sq[0:TC, 4:8], sq[0:TC, 0:4], AF.Ln, [sqb, C], [sqb], scale=1.0 / 128.0, bias=epsb[0:TC, 1:2])
            act(sq[0:TC, 4:8], sq[0:TC, 4:8], AF.Exp, [sqb], [sqb], scale=-0.5)
            on, onb = onr.next()
            tt(on[0:TC], obk[0:TC, :].rearrange("p (h d) -> p h d", h=4), sq[0:TC, 4:8].unsqueeze(2).to_broadcast([TC, 4, 128]),
               ALU.mult, [obb, sqb], [onb])
            nb_release("F_o")
            yield
            bk, bb = nb()
            bkb = bk.bitcast(BF16)
            for h in range(4):
                tr(bkb[:, h * TC:(h + 1) * TC], on[0:TC, h, :], identb[0:TC, 0:TC], [onb, C], [bb], h == 3)
            stt(mixT[:, 0:4, tsl(tc)], bkb[:, 0:4 * TC].rearrange("p (h t) -> p h t", h=4), gnorm, shg[:, :, tsl(tc)],
                ALU.mult, ALU.mult, [bb, C, A_bufs["shg"]], [mixTb[0]])
            yield

        if not last and not is_sample:
            act(KT[:, :, 0:128], KT[:, :, TT:TT + 128], AF.Copy, [KTb], [KTb])
            act(Vaug[:, 0, :, 0:64], Vaug[:, NTC, :, 0:64], AF.Copy, [Vaugb[NTC]], [Vaugb[0]])

        yield "SPLIT"
        def res_x_reload(tc, half):
            xh, xhb = xhalf.next()
            S.dma("sp", xh[0:TC, :], x_rows(tc)[:, half * 512:(half + 1) * 512], writes=[xhb])
            return xh[0:TC, :], xhb
        yield from out_proj_residual(tcs, TC, lambda kc, tc: mixT[:, kc, tsl(tc)], mixTb, WO0, res_x_reload)
        x1T, x1Tb = actT_b
        ln1_def = []
        yield from layer_norm(tcs, TC, 0, x1T, x1Tb, ln1_def)

        for n_ in ("qmT", "PTm", "omT"):
            S.inherit(CD_bufs[n_], [CD_bufs["hT"]] + stg3b)

        for j in range(2):
            wv, wb = wload(WQ0 + j, 8, 512)
            for i in range(4):
                bk, bb = nb()
                for kc in range(8):
                    mm(bk[:, 0:TT], wv[:, kc, i * 128:(i + 1) * 128], x1T[:, kc, 0:TT], kc == 0, kc == 7, [wb, x1Tb[kc]], [bb], kc == 7)
                act(qmT[:, j * 4 + i, 0:TT], bk[:, 0:TT], AF.Copy, [bb], [CD_bufs["qmT"]], scale=1.0 / 16.0)
                yield
        for h in range(4):
            for mt in range(2):
                bk, bb = nb()
                for dc in range(2):
                    mm(bk[:, 0:TT], mkT[:, h * 2 + dc, mt * 128:(mt + 1) * 128], qmT[:, h * 2 + dc, 0:TT], dc == 0, dc == 1,
                       [mkTb, CD_bufs["qmT"]], [bb], dc == 1)
                act(PTm[:, mt, h, 0:TT], bk[:, 0:TT], AF.Exp, [bb], [CD_bufs["PTm"]])
            if ln1_def:
                ln1_def.pop(0)()
            yield
        for tc in tcs:
            yield
            obanks = [nb(), nb()]
            dbk, dbb = nb()
            for h in range(4):
                obk, obb = obanks[h // 2]
                for mt in range(2):
                    mm(obk[0:TC, (h % 2) * 256:(h % 2 + 1) * 256], PTm[:, mt, h, tsl(tc)], mvb[:, mt, h * 256:(h + 1) * 256],
                       mt == 0, mt == 1, [CD_bufs["PTm"], mvbb], [obb], h % 2 == 1 and mt == 1)
                for mt in range(2):
                    mm(dbk[0:TC, h:h + 1], PTm[:, mt, h, tsl(tc)], onescol[:, 0:1], mt == 0, mt == 1,
                       [CD_bufs["PTm"], C], [dbb], h == 3 and mt == 1)
            sq, sqb = statr.next()
            dve(lambda e, sq=sq, dbk=dbk: e.reciprocal(out=sq[0:TC, 0:4], in_=dbk[0:TC, 0:4]), [dbb], [sqb])
            om, omb = omr.next()
            for h in range(4):
                obk, obb = obanks[h // 2]
                act(om[0:TC, h * 256:(h + 1) * 256], obk[0:TC, (h % 2) * 256:(h % 2 + 1) * 256], AF.Identity, [obb, sqb], [omb],
                    scale=sq[0:TC, h:h + 1])
            yield
            bk, bb = nb()
            bkb = bk.bitcast(BF16)
            for kc in range(8):
                tr(bkb[:, kc * TC:(kc + 1) * TC], om[0:TC, kc * 128:(kc + 1) * 128], identb[0:TC, 0:TC], [omb, C], [bb], kc == 7)
            act(omT[:, :, tsl(tc)], bkb[:, 0:8 * TC].rearrange("p (k t) -> p k t", k=8), AF.Copy, [bb], [CD_bufs["omT"]])
        for d_ in ln1_def:
            d_()
        ln1_def.clear()
        yield from out_proj_residual(tcs, TC, lambda kc, tc: omT[:, kc, tsl(tc)], [CD_bufs["omT"]], WMO0,
                                     lambda tc, half: (resid[0:TC, tc, half * 512:(half + 1) * 512], residb[tc]))
        yield "CDONE"
        x2T, x2Tb = actT_b
        ln2_def = []
        yield from layer_norm(tcs, TC, 1, x2T, x2Tb, ln2_def)

        S.inherit(CD_bufs["hT"], [CD_bufs["qmT"], CD_bufs["PTm"], CD_bufs["omT"]] + stg3b)
        for j in range(11):
            yield
            wv, wb = wload(WF0 + j, 8, 512)
            for jj in range(2):
                cidx = 2 * j + jj
                gbk, gbb = nb()
                for kc in range(8):
                    mm(gbk[:, 0:TT], wv[:, kc, jj * 128:(jj + 1) * 128], x2T[:, kc, 0:TT], kc == 0, kc == 7, [wb, x2Tb[kc]], [gbb], kc == 7)
                ubk, ubb = nb()
                for kc in range(8):
                    mm(ubk[:, 0:TT], wv[:, kc, 256 + jj * 128:256 + (jj + 1) * 128], x2T[:, kc, 0:TT], kc == 0, kc == 7, [wb, x2Tb[kc]], [ubb], kc == 7)
                sg_, sgb = sgr.next()
                act(sg_[:, 0:TT], gbk[:, 0:TT], AF.Silu, [gbb], [sgb])
                tt(hT[:, cidx, 0:TT], sg_[:, 0:TT], ubk[:, 0:TT], ALU.mult, [sgb, ubb], [CD_bufs["hT"]])
                if ln2_def and cidx % 4 == 3:
                    ln2_def.pop(0)()
                yield
        for d_ in ln2_def:
            d_()
        ln2_def.clear()
        for half in range(2):
            obanks = [nb(hold=("B_ffo", tc_)) for tc_ in tcs]
            for gi, (k0, k1) in enumerate(FFO_GROUPS):
                yield
                wv, wb = wload(WFO0 + half * 3 + gi, k1 - k0, 512)
                for tc in tcs:
                    yield
                    obk, obb = obanks[tc]
                    for kc in range(k0, k1):
                        mm(obk[0:TC, :], hT[:, kc, tsl(tc)], wv[:, kc - k0, :], kc == 0, kc == 21, [CD_bufs["hT"], wb], [obb],
                           kc == k1 - 1)
            for tc in tcs:
                obk, obb = obanks[tc]
                r = resid[0:TC, tc, half * 512:(half + 1) * 512]
                stt(r, r, ALPHA, obk[0:TC, :], ALU.mult, ALU.add, [residb[tc], obb], [residb[tc]])
                half_stats(TC, tc, half)
                nb_release(("B_ffo", tc))
        yield from layer_norm(tcs, TC, 2, None, None, None)
        for tc in tcs:
            out_toks.append(S.dma("act", y_rows(tc), stg3[0:TC, tc, :], reads=[stg3b[tc]]))

    def drive(entries):
        prev = None
        for g, pre, post in entries:
            ratio = 1.55
            if pre is not None:
                if prev is not None:
                    for tag in prev:
                        if tag == "CDONE":
                            break
                    else:
                        prev = None
                    ratio = 0.8
                pre()
            front_done = False
            if prev is not None:
                acc = 0.0
                for _ in prev:
                    acc += 1.0 / ratio
                    while acc >= 1.0 and not front_done:
                        acc -= 1.0
                        if next(g) == "SPLIT":
                            front_done = True
            while not front_done:
                if next(g) == "SPLIT":
                    front_done = True
            if post is not None:
                post()
            prev = g
        if prev is not None:
            for _ in prev:
                pass

    def main_program():
        chk(0)
        entries = []
        for b in range(NB):
            def pre(b=b):
                mem_prepare(memp[b * MEM:(b + 1) * MEM, :], None, mkp[b * MEM:(b + 1) * MEM, :], mvp[b * MEM:(b + 1) * MEM, :], True)
                dve(lambda e: e.memset(Sst[:], 0.0), [], [Sb])
                act(Sbf[gch[0] % 3][0], Sst[:], AF.Copy, [Sb], [Sbf[gch[0] % 3][1]])

            def post(b=b):
                out_toks.append(S.dma("act", sp_o[b].rearrange("h k v -> k h v"), Sst[:], reads=[Sb]))
            for t in range(NTILE):
                r0 = b * SEQ + t * TTP
                last = (t == NTILE - 1)
                g = emit_tile(TTP, 128, 64,
                              lambda tc, r0=r0: xp[r0 + tc * 128:r0 + (tc + 1) * 128, :],
                              lambda tc, r0=r0: yp[r0 + tc * 128:r0 + (tc + 1) * 128, :],
                              t == 0, last, False, (kp[b], vp[b]) if last else None)
                entries.append((g, pre if t == 0 else None, post if last else None))

        if sample:
            def pre_s():
                mem_prepare(cmk, cmv, None, None, False)
                S.dma("sp", Sst[:], sh.rearrange("h k v -> k h v"), writes=[Sb])
                act(Sbf[gch[0] % 3][0], Sst[:], AF.Copy, [Sb], [Sbf[gch[0] % 3][1]])
                xt, xtb = xpool.next()
                ck3 = ck.rearrange("t (k d) -> t k d", k=2)
                for kvh in range(2):
                    for dup in range(2):
                        S.dma("sp", xt[:, kvh * 128 + dup * 64:kvh * 128 + (dup + 1) * 64], ck3[:, kvh, :], writes=[xtb])
                S.dma("sp", xt[:, 256:384], cv, writes=[xtb])
                bk, bb = nb()
                for kvh in range(2):
                    tr(bk[:, kvh * 128:(kvh + 1) * 128], xt[:, kvh * 128:(kvh + 1) * 128], identf[:], [xtb, C], [bb], kvh == 1)
                act(KT[:, :, 0:128], bk[:, 0:256].rearrange("p (k t) -> p k t", k=2), AF.Copy, [bb], [KTb])
                act(Vaug[:, 0, :, 0:64], xt[:, 256:384].rearrange("p (k d) -> p k d", k=2), AF.Copy, [xtb], [Vaugb[0]])

            def post_s():
                out_toks.append(S.dma("act", ss_o.rearrange("h k v -> k h v"), Sst[:], reads=[Sb]))
            entries.append((emit_tile(16, 16, 16, lambda tc: xs[:, :], lambda tc: ys[:, :], False, True, True, (ks_o, vs_o)),
                            pre_s, post_s))
        drive(entries)

    try:
        main_program()
    except _Stop:
        pass
    S.wait_all("act", out_toks + [b_.w for b_ in wsb if b_.w is not None])
    S.emit(st)
    st.close()
    return nc


_CACHE = {}


def _get_nc(NB, SEQ):
    key = (NB, SEQ)
    if key not in _CACHE:
        _CACHE[key] = build(NB, SEQ)
    return _CACHE[key]


def run(inputs, n_cores=N_CORES):
    f32 = lambda a: np.ascontiguousarray(np.asarray(a, dtype=np.float32))
    x_prompt = f32(inputs["x_prompt"])
    B, SEQ, _ = x_prompt.shape
    NB = B // n_cores
    nc = _get_nc(NB, SEQ)
    consts = _consts()
    shared = {
        "w_in": f32(inputs["w_in"][0]), "lbl": f32(inputs["hgrn_lb_logits"]), "gn": f32(inputs["hgrn_norm_g"]),
        "sinks": f32(inputs["attn_sinks"]), "w_out": f32(inputs["w_out"][0]), "w_mq": f32(inputs["w_mem_q"][0]),
        "w_mkv": f32(inputs["w_mem_kv"][0]), "w_mo": f32(inputs["w_mem_o"][0]), "w_fi": f32(inputs["w_ffn_in"][0]),
        "w_fo": f32(inputs["w_ffn_out"][0]), "ln_g": f32(inputs["ln_g"][0]), "ln_b": f32(inputs["ln_b"][0]),
    }
    shared.update(consts)
    mem_prompt = f32(inputs["mem_prompt"])
    x_sample = f32(inputs["x_sample"])
    ck = f32(inputs["cache_swa_k"][0]); cv = f32(inputs["cache_swa_v"][0])
    shs = f32(inputs["state_hgrn"][0])
    cmk = f32(inputs["cache_mem_k"][0]); cmv = f32(inputs["cache_mem_v"][0])
    in_maps = []
    for c in range(n_cores):
        m = dict(shared)
        m["xp"] = x_prompt[c * NB:(c + 1) * NB].reshape(NB * SEQ, D)
        m["memp"] = mem_prompt[c * NB:(c + 1) * NB].reshape(NB * MEM, D)
        m["xs"] = x_sample[c].reshape(16, D)
        m["ck"] = ck[c].reshape(128, 128)
        m["cv"] = cv[c].reshape(128, 128)
        m["sh"] = shs[c]
        m["cmk"] = cmk[c].reshape(MEM, D)
        m["cmv"] = cmv[c].reshape(MEM, D)
        in_maps.append(m)
    res = run_bass_kernel_spmd(nc, in_maps, core_ids=list(range(n_cores)))
    R = res.results
    cat = lambda name, shp: np.concatenate([np.asarray(r[name], dtype=np.float32).reshape(shp) for r in R], axis=0)
    yp = cat("yp", (NB, SEQ, D))
    ys = cat("ys", (1, 16, D))
    kp = cat("kp", (NB, 128, 2, 64))[None]
    vp = cat("vp", (NB, 128, 2, 64))[None]
    sp_o = cat("sp_o", (NB, 4, 128, 128))[None]
    mkp = cat("mkp", (NB, MEM, 4, 256))[None]
    mvp = cat("mvp", (NB, MEM, 4, 256))[None]
    ks = cat("ks_o", (1, 16, 2, 64))[None]
    vs = cat("vs_o", (1, 16, 2, 64))[None]
    ss = cat("ss_o", (1, 4, 128, 128))[None]
    return (yp, ys, kp, vp, sp_o, mkp, mvp, ks, vs, ss)


def kernel(**inputs):
    return run(inputs)
```
